# Optimizing a Trainium2 kernel written in Bass

```python
import jax
import jax.numpy as jnp
from jax import lax
import numpy as np

D_MODEL = 2048
BATCH = 2
SEQ = 16384
DEPTH = 2

MEM_LEN = 256
D_FF = 5504
EPS = 1e-6
NEG = -1e30
FORCED = 1e9
Q_BLOCK = 128

NSA_HEADS = 6
NSA_GROUPS = 2
NSA_HEAD_DIM = 128
CMP_LEN = 32
CMP_STRIDE = 16
CMP_HIDDEN = 256
SEL_LEN = 64
SEL_TOPN = 16
WIN_LEN = 512

DIL_PATTERNS = ((128, 1), (512, 4), (2048, 16))
DIL_GROUPS = 3
DIL_HEADS = 4
DIL_HEAD_DIM = 64

MEM_HEADS = 4
MEM_HEAD_DIM = 128

N_ALIBI = NSA_HEADS + DIL_GROUPS * DIL_HEADS
NSA_Q = NSA_HEADS * NSA_HEAD_DIM
NSA_KV = NSA_GROUPS * NSA_HEAD_DIM
DIL_W = DIL_GROUPS * DIL_HEADS * DIL_HEAD_DIM
DIL_OUT = DIL_HEADS * DIL_HEAD_DIM
MEM_Q = MEM_HEADS * MEM_HEAD_DIM
IN_SIZES = (NSA_Q,) + (NSA_KV,) * 6 + (3 * NSA_HEADS,) + (DIL_W,) * 3 + (MEM_Q,) + (D_MODEL,) * 3
N_IN = sum(IN_SIZES)

kernel_name = 'hybrid_nsa_dilated_memory_macaron'


def rms_norm(x, g):
    xf = x.astype(jnp.float32)
    y = xf * lax.rsqrt(jnp.mean(xf * xf, axis=-1, keepdims=True) + EPS)
    return (y * g.astype(jnp.float32)).astype(x.dtype)


def swiglu(x, w_gate, w_up, w_down):
    return (jax.nn.silu(x @ w_gate) * (x @ w_up)) @ w_down


def masked_softmax(s, mask):
    s = jnp.where(mask, s, NEG)
    m = jnp.max(s, axis=-1, keepdims=True)
    e = jnp.where(mask, jnp.exp(s - m), 0.0)
    den = jnp.sum(e, axis=-1, keepdims=True)
    p = e / jnp.maximum(den, 1e-30)
    lse = m[..., 0] + jnp.log(jnp.maximum(den[..., 0], 1e-30))
    return p, lse


def alibi_slopes():
    slopes = 2.0 ** (-8.0 * jnp.arange(1, N_ALIBI + 1, dtype=jnp.float32) / N_ALIBI)
    idx = np.arange(N_ALIBI)
    nsa_idx = idx[::N_ALIBI // NSA_HEADS][:NSA_HEADS]
    dil_idx = np.setdiff1d(idx, nsa_idx)
    return slopes[nsa_idx], slopes[dil_idx].reshape(DIL_GROUPS, DIL_HEADS)


def nsa_compress(k, pe, w1, w2):
    B, S, G, dh = k.shape
    ch = k.reshape(B, S // CMP_STRIDE, CMP_STRIDE, G, dh)
    blocks = jnp.concatenate([ch[:, :-1], ch[:, 1:]], axis=2) + pe[None, None, :, None, :]
    flat = blocks.transpose(0, 1, 3, 2, 4).reshape(B, -1, G, CMP_LEN * dh)
    return jax.nn.silu(flat @ w1) @ w2


def nsa_attention(q, k_cmp, v_cmp, k_slc, v_slc, k_win, v_win, gates, slopes):
    B, S, H, dh = q.shape
    G = k_slc.shape[2]
    R = H // G
    f32 = jnp.float32
    n_cmp = k_cmp.shape[1]
    n_blk = S // SEL_LEN
    n_top = min(SEL_TOPN, n_blk)
    ratio = SEL_LEN // CMP_STRIDE
    scale = dh ** -0.5
    cmp_end = jnp.arange(n_cmp) * CMP_STRIDE + CMP_LEN - 1
    slopes_g = slopes.reshape(G, R)[None, :, :, None, None]
    kb = k_slc.reshape(B, n_blk, SEL_LEN, G, dh).transpose(0, 3, 1, 2, 4)
    vb = v_slc.reshape(B, n_blk, SEL_LEN, G, dh).transpose(0, 3, 1, 2, 4)
    kw = jnp.pad(k_win, ((0, 0), (WIN_LEN, 0), (0, 0), (0, 0)))
    vw = jnp.pad(v_win, ((0, 0), (WIN_LEN, 0), (0, 0), (0, 0)))
    n_qb = S // Q_BLOCK
    qb = q.reshape(B, n_qb, Q_BLOCK, G, R, dh).transpose(1, 0, 3, 4, 2, 5)
    gb = gates.reshape(B, n_qb, Q_BLOCK, G, R, 3).transpose(1, 0, 3, 4, 2, 5)
    b_idx = jnp.arange(B)[:, None, None, None]
    g_idx = jnp.arange(G)[None, :, None, None]
    blk = jnp.arange(n_blk)

    def one_block(args):
        qi, gi, i = args
        t = i * Q_BLOCK + jnp.arange(Q_BLOCK)
        dist_c = t[:, None] - cmp_end[None, :]
        s = jnp.einsum('bgrqd,bcgd->bgrqc', qi, k_cmp).astype(f32) * scale - slopes_g * dist_c.astype(f32)
        p_cmp, _ = masked_softmax(s, dist_c >= 0)
        o_cmp = jnp.einsum('bgrqc,bcgd->bgrqd', p_cmp.astype(v_cmp.dtype), v_cmp)
        imp = jnp.pad(jnp.sum(p_cmp, axis=2), ((0, 0), (0, 0), (0, 0), (1, ratio * n_blk + ratio - n_cmp - 1)))
        quad = imp[..., :ratio * n_blk].reshape(B, G, Q_BLOCK, n_blk, ratio)
        score = (0.5 * quad[..., 0] + quad[..., 1] + quad[..., 2] + quad[..., 3]
                 + 0.5 * imp[..., ratio:ratio * n_blk + ratio:ratio])
        cur = (t // SEL_LEN)[:, None]
        forced = (blk[None, :] == 0) | (blk[None, :] == cur) | (blk[None, :] == cur - 1)
        score = jnp.where(forced, FORCED, jnp.where(blk[None, :] <= cur, score, NEG))
        _, sel = lax.top_k(score, n_top)
        ks = kb[b_idx, g_idx, sel].reshape(B, G, Q_BLOCK, n_top * SEL_LEN, dh)
        vs = vb[b_idx, g_idx, sel].reshape(B, G, Q_BLOCK, n_top * SEL_LEN, dh)
        pos = (sel[..., None] * SEL_LEN + jnp.arange(SEL_LEN)).reshape(B, G, 1, Q_BLOCK, n_top * SEL_LEN)
        dist_s = t[:, None] - pos
        s = jnp.einsum('bgrqd,bgqkd->bgrqk', qi, ks).astype(f32) * scale - slopes_g * dist_s.astype(f32)
        p_slc, _ = masked_softmax(s, dist_s >= 0)
        o_slc = jnp.einsum('bgrqk,bgqkd->bgrqd', p_slc.astype(vs.dtype), vs)
        start = i * Q_BLOCK
        kwi = lax.dynamic_slice_in_dim(kw, start, WIN_LEN + Q_BLOCK, axis=1)
        vwi = lax.dynamic_slice_in_dim(vw, start, WIN_LEN + Q_BLOCK, axis=1)
        spos = start - WIN_LEN + jnp.arange(WIN_LEN + Q_BLOCK)
        dist_w = t[:, None] - spos[None, :]
        mask_w = (dist_w >= 0) & (dist_w < WIN_LEN) & (spos[None, :] >= 0)
        s = jnp.einsum('bgrqd,bkgd->bgrqk', qi, kwi).astype(f32) * scale - slopes_g * dist_w.astype(f32)
        p_win, _ = masked_softmax(s, mask_w)
        o_win = jnp.einsum('bgrqk,bkgd->bgrqd', p_win.astype(vwi.dtype), vwi)
        o = gi[..., 0:1] * o_cmp + gi[..., 1:2] * o_slc + gi[..., 2:3] * o_win
        return o.transpose(0, 3, 1, 2, 4).reshape(B, Q_BLOCK, H * dh)

    out = lax.map(one_block, (qb, gb, jnp.arange(n_qb)))
    return out.transpose(1, 0, 2, 3).reshape(B, S, H * dh)


def dilated_group(q, k, v, window, dilation, slopes):
    B, S, H, dh = q.shape
    f32 = jnp.float32
    blk = window // dilation
    unit = blk * dilation
    s_pad = -(-S // unit) * unit
    n_sub = s_pad // dilation
    nb = n_sub // blk

    def split(x):
        x = jnp.pad(x, ((0, 0), (0, s_pad - S), (0, 0), (0, 0)))
        return x.reshape(B, n_sub, dilation, H, dh).transpose(0, 2, 1, 3, 4).reshape(B, dilation, nb, blk, H, dh)

    def with_prev(x):
        prev = jnp.pad(x, ((0, 0), (0, 0), (1, 0), (0, 0), (0, 0), (0, 0)))[:, :, :-1]
        return jnp.concatenate([prev, x], axis=3)

    qs = split(q)
    kk = with_prev(split(k))
    vv = with_prev(split(v))
    s = jnp.einsum('bdnqhe,bdnkhe->bdnhqk', qs, kk).astype(f32) * dh ** -0.5
    kidx = jnp.arange(2 * blk)
    rel = blk + jnp.arange(blk)[:, None] - kidx[None, :]
    first = (jnp.arange(nb) == 0)[:, None, None] & (kidx < blk)[None, None, :]
    mask = (rel >= 0)[None] & (rel <= blk)[None] & ~first
    s = s - slopes[:, None, None] * (rel * dilation).astype(f32)
    p, lse = masked_softmax(s, mask[None, None, :, None])
    o = jnp.einsum('bdnhqk,bdnkhe->bdnqhe', p.astype(v.dtype), vv)
    o = o.reshape(B, dilation, n_sub, H, dh).transpose(0, 2, 1, 3, 4).reshape(B, s_pad, H, dh)[:, :S]
    lse = lse.transpose(0, 1, 2, 4, 3).reshape(B, dilation, n_sub, H).transpose(0, 2, 1, 3).reshape(B, s_pad, H)[:, :S]
    return o, lse


def dilated_attention(q, k, v, slopes):
    B, S = q.shape[:2]
    outs, lses = [], []
    for gi, (w, d) in enumerate(DIL_PATTERNS):
        o, l = dilated_group(q[:, :, gi], k[:, :, gi], v[:, :, gi], w, d, slopes[gi])
        outs.append(o)
        lses.append(l)
    wts = jax.nn.softmax(jnp.stack(lses, axis=0), axis=0)
    o = jnp.einsum('gbsh,gbshe->bshe', wts, jnp.stack(outs, axis=0).astype(jnp.float32))
    return o.astype(q.dtype).reshape(B, S, DIL_OUT)


def memory_attention(q, mk, mv):
    B, S, H, dh = q.shape
    s = jnp.einsum('bshd,bmhd->bhsm', q, mk).astype(jnp.float32) * dh ** -0.5
    p = jax.nn.softmax(s, axis=-1)
    return jnp.einsum('bhsm,bmhd->bshd', p.astype(mv.dtype), mv).reshape(B, S, H * dh)


def hybrid_mixer(u, mem, w_in, cmp_pe_k, cmp_pe_v, cmp_k_w1, cmp_k_w2, cmp_v_w1, cmp_v_w2,
                 mem_norm_g, w_mem_kv, w_up_nsa, w_up_dil, w_up_mem, w_out):
    B, S, _ = u.shape
    offsets = np.cumsum(IN_SIZES)[:-1].tolist()
    (q_a, kc, vc, ks, vs, kw, vw, g_nsa, q_b, k_b, v_b, q_m,
     g_a, g_b, g_m) = jnp.split(u @ w_in, offsets, axis=-1)
    slope_nsa, slope_dil = alibi_slopes()
    grp = lambda t: t.reshape(B, S, NSA_GROUPS, NSA_HEAD_DIM)
    k_cmp = nsa_compress(grp(kc), cmp_pe_k, cmp_k_w1, cmp_k_w2)
    v_cmp = nsa_compress(grp(vc), cmp_pe_v, cmp_v_w1, cmp_v_w2)
    y_a = nsa_attention(q_a.reshape(B, S, NSA_HEADS, NSA_HEAD_DIM), k_cmp, v_cmp,
                        grp(ks), grp(vs), grp(kw), grp(vw),
                        jax.nn.sigmoid(g_nsa.reshape(B, S, NSA_HEADS, 3)), slope_nsa)
    dil = lambda t: t.reshape(B, S, DIL_GROUPS, DIL_HEADS, DIL_HEAD_DIM)
    y_b = dilated_attention(dil(q_b), dil(k_b), dil(v_b), slope_dil)
    M = mem.shape[1]
    mk, mv = jnp.split(rms_norm(mem, mem_norm_g) @ w_mem_kv, 2, axis=-1)
    y_m = memory_attention(q_m.reshape(B, S, MEM_HEADS, MEM_HEAD_DIM),
                           mk.reshape(B, M, MEM_HEADS, MEM_HEAD_DIM),
                           mv.reshape(B, M, MEM_HEADS, MEM_HEAD_DIM))
    merged = (jax.nn.sigmoid(g_a) * (y_a @ w_up_nsa)
              + jax.nn.sigmoid(g_b) * (y_b @ w_up_dil)
              + jax.nn.sigmoid(g_m) * (y_m @ w_up_mem))
    return merged @ w_out


def setup_inputs(seed: int = 0) -> dict:
    key = jax.random.key(seed)
    keys = iter(jax.random.split(key, 40))
    L, D, F = DEPTH, D_MODEL, D_FF

    def w(shape, fan_in):
        return jax.random.normal(next(keys), shape, jnp.float32) * fan_in ** -0.5

    def gain():
        return 1.0 + 0.02 * jax.random.normal(next(keys), (L, D), jnp.float32)

    cmp_in = CMP_LEN * NSA_HEAD_DIM
    return {
        'x': jax.random.normal(next(keys), (BATCH, SEQ, D), jnp.float32),
        'mem': jax.random.normal(next(keys), (BATCH, MEM_LEN, D), jnp.float32),
        'ffn1_pre_g': gain(),
        'ffn1_w_gate': w((L, D, F), D),
        'ffn1_w_up': w((L, D, F), D),
        'ffn1_w_down': w((L, F, D), F),
        'ffn1_post_g': gain(),
        'mix_pre_g': gain(),
        'w_in': w((L, D, N_IN), D),
        'cmp_pe_k': 0.1 * jax.random.normal(next(keys), (L, CMP_LEN, NSA_HEAD_DIM), jnp.float32),
        'cmp_pe_v': 0.1 * jax.random.normal(next(keys), (L, CMP_LEN, NSA_HEAD_DIM), jnp.float32),
        'cmp_k_w1': w((L, cmp_in, CMP_HIDDEN), cmp_in),
        'cmp_k_w2': w((L, CMP_HIDDEN, NSA_HEAD_DIM), CMP_HIDDEN),
        'cmp_v_w1': w((L, cmp_in, CMP_HIDDEN), cmp_in),
        'cmp_v_w2': w((L, CMP_HIDDEN, NSA_HEAD_DIM), CMP_HIDDEN),
        'mem_norm_g': gain(),
        'w_mem_kv': w((L, D, 2 * MEM_Q), D),
        'w_up_nsa': w((L, NSA_Q, D), NSA_Q),
        'w_up_dil': w((L, DIL_OUT, D), DIL_OUT),
        'w_up_mem': w((L, MEM_Q, D), MEM_Q),
        'w_out': w((L, D, D), D),
        'mix_post_g': gain(),
        'ffn2_pre_g': gain(),
        'ffn2_w_gate': w((L, D, F), D),
        'ffn2_w_up': w((L, D, F), D),
        'ffn2_w_down': w((L, F, D), F),
        'ffn2_post_g': gain(),
    }


def reference(x, mem, ffn1_pre_g, ffn1_w_gate, ffn1_w_up, ffn1_w_down, ffn1_post_g,
              mix_pre_g, w_in, cmp_pe_k, cmp_pe_v, cmp_k_w1, cmp_k_w2, cmp_v_w1, cmp_v_w2,
              mem_norm_g, w_mem_kv, w_up_nsa, w_up_dil, w_up_mem, w_out, mix_post_g,
              ffn2_pre_g, ffn2_w_gate, ffn2_w_up, ffn2_w_down, ffn2_post_g):
    h = x
    for l in range(DEPTH):
        f1 = swiglu(rms_norm(h, ffn1_pre_g[l]), ffn1_w_gate[l], ffn1_w_up[l], ffn1_w_down[l])
        h = h + 0.5 * rms_norm(f1, ffn1_post_g[l])
        mix = hybrid_mixer(rms_norm(h, mix_pre_g[l]), mem, w_in[l], cmp_pe_k[l], cmp_pe_v[l],
                           cmp_k_w1[l], cmp_k_w2[l], cmp_v_w1[l], cmp_v_w2[l], mem_norm_g[l],
                           w_mem_kv[l], w_up_nsa[l], w_up_dil[l], w_up_mem[l], w_out[l])
        h = h + rms_norm(mix, mix_post_g[l])
        f2 = swiglu(rms_norm(h, ffn2_pre_g[l]), ffn2_w_gate[l], ffn2_w_up[l], ffn2_w_down[l])
        h = h + 0.5 * rms_norm(f2, ffn2_post_g[l])
    return h
```

```python
import numpy as np
from contextlib import ExitStack
import concourse.bass as bass
import concourse.mybir as mybir
from concourse.bass_utils import run_bass_kernel_spmd

F32 = mybir.dt.float32
F32R = mybir.dt.float32r
AF = mybir.ActivationFunctionType
ALU = mybir.AluOpType

D = 2048
NCH = 16
DFF = 5504
NF = 43
EPS = 1e-6
TT = 512
NDS = 8

FM = {}
_o = 0
for _n, _k in (("qa", 6), ("kc", 2), ("vc", 2), ("ks", 2), ("kw", 2), ("qb", 6), ("kb", 6),
               ("qm", 4), ("gn", 1), ("ga", 16), ("gb", 16), ("gm", 16)):
    FM[_n] = (_o, _k)
    _o += _k
NFM = _o
NVM = 1280
KSIDE = ("kc", "vc", "ks", "kw", "kb")
FM_K = {"kc": 0, "vc": 2, "ks": 4, "kw": 6, "kb": 8}


class Buf:
    __slots__ = ("w", "r", "ap")

    def __init__(self, ap=None):
        self.w = None
        self.r = {}
        self.ap = ap


class Ctx:
    def __init__(self, nc, es):
        self.nc = nc
        self.es = es
        self.engs = {"pe": nc.tensor, "dve": nc.vector, "act": nc.scalar, "pool": nc.gpsimd, "sp": nc.sync}
        self.esem = {k: es.enter_context(nc.semaphore("s_" + k)) for k in self.engs}
        self.ecnt = {k: 0 for k in self.engs}
        self.dq = {q: [[es.enter_context(nc.semaphore("d_%s%d" % (q, i))), 0] for i in range(NDS)]
                   for q in ("sp", "pool")}
        self.dqi = {q: 0 for q in self.dq}
        self.seen = {}
        self.pend = {k: ([], []) for k in self.engs}
        self.outs = []
        self.mute = False

    def sb(self, name, shape, dt=F32):
        t = self.es.enter_context(self.nc.sbuf_tensor(name, list(shape), dt))
        return t

    def ps(self, name, shape=(128, 512)):
        return self.es.enter_context(self.nc.psum_tensor(name, list(shape), F32))

    def _wait(self, eng, evs):
        best = {}
        for s, v in evs:
            if best.get(s, 0) < v:
                best[s] = v
        for s, v in best.items():
            k = (eng, s)
            if self.seen.get(k, 0) < v:
                self.engs[eng].wait_ge(s, v)
                self.seen[k] = v

    def _deps(self, reads, writes):
        evs = []
        for b in reads:
            if b.w is not None:
                evs.append(b.w)
        for b in writes:
            if b.w is not None:
                evs.append(b.w)
            evs.extend(b.r.items())
        return evs

    def op(self, eng, fn, reads=(), writes=(), signal=True):
        if self.mute:
            return None
        self._wait(eng, self._deps(reads, writes))
        ins = fn(self.engs[eng])
        pr, pw = self.pend[eng]
        pr.extend(reads)
        pw.extend(writes)
        if signal:
            self.ecnt[eng] += 1
            s = self.esem[eng]
            v = self.ecnt[eng]
            ins.then_inc(s, 1)
            for b in pr:
                b.r[s] = v
            for b in pw:
                b.w = (s, v)
                b.r = {}
            pr.clear()
            pw.clear()
        return ins

    def dma(self, q, out, in_, reads=(), writes=(), final=False):
        if self.mute:
            return None
        evs = self._deps(reads, writes)
        slot = self.dq[q][self.dqi[q] % NDS]
        self.dqi[q] += 1
        if slot[1] > 0:
            evs.append((slot[0], slot[1]))
        self._wait(q, evs)
        ins = self.engs[q].dma_start(out=out, in_=in_)
        slot[1] += 16
        ins.then_inc(slot[0], 16)
        for b in reads:
            b.r[slot[0]] = slot[1]
        for b in writes:
            b.w = (slot[0], slot[1])
            b.r = {}
        if final:
            self.outs.append((slot[0], slot[1]))

    def barrier(self):
        evs = [(self.esem[k], self.ecnt[k]) for k in self.engs if self.ecnt[k] > 0]
        for q in self.dq:
            for sl in self.dq[q]:
                if sl[1] > 0:
                    evs.append((sl[0], sl[1]))
        for k in self.engs:
            self._wait(k, evs)

    def finish(self):
        self.barrier()


def r_(ap):
    return ap.bitcast(F32R)


class Common:
    def __init__(self, cx, consts_ap):
        nc = cx.nc
        self.cx = cx
        self.ones = Buf(cx.sb("ones", (128, 128), F32R))
        self.epsc = Buf(cx.sb("epsc", (128, 1), F32))
        self.ones32 = Buf(cx.sb("ones32", (128, 128), F32))
        cx.op("pool", lambda e: e.memset(self.ones32.ap[:], 1.0), writes=[self.ones32])
        cx.op("dve", lambda e: e.tensor_copy(out=self.ones.ap[:], in_=self.ones32.ap[:]),
              reads=[self.ones32], writes=[self.ones])
        cx.op("pool", lambda e: e.memset(self.epsc.ap[:], EPS), writes=[self.epsc])
        self.PS = [Buf(cx.ps("ps%d" % i)) for i in range(8)]
        self.sq = [Buf(cx.sb("sq%d" % i, (128, TT), F32R)) for i in range(2)]
        self.rt = Buf(cx.sb("rt", (128, TT), F32))


def rms_rstd(cm, src, srcbuf, rstd, psb, ncols=TT, nch=NCH):
    cx = cm.cx
    for c in range(nch):
        sq = cm.sq[c % 2]
        cx.op("act", lambda e: e.activation(out=sq.ap[:, :ncols], in_=src[:, c, :], func=AF.Square),
              reads=[srcbuf], writes=[sq])
        cx.op("pe", lambda e: e.matmul(psb.ap[:, :ncols], cm.ones.ap[:], sq.ap[:, :ncols],
                                       start=(c == 0), stop=(c == nch - 1)),
              reads=[sq, cm.ones], writes=[psb])
    cx.op("act", lambda e: e.activation(out=cm.rt.ap[:, :ncols], in_=psb.ap[:, :ncols], func=AF.Sqrt,
                                        scale=1.0 / (nch * 128), bias=cm.epsc.ap[:]),
          reads=[psb, cm.epsc], writes=[cm.rt])
    cx.op("dve", lambda e: e.reciprocal(out=rstd.ap[:, :ncols], in_=cm.rt.ap[:, :ncols]),
          reads=[cm.rt], writes=[rstd])


class FFN:
    def __init__(self, cx, cm):
        self.cx = cx
        self.cm = cm
        self.xn = Buf(cx.sb("xn", (128, NCH, TT), F32R))
        self.hy = Buf(cx.sb("hy", (128, NCH, TT), F32))
        self.act = [Buf() for _ in range(22)]
        self.act_t = cx.sb("actT", (128, 22, TT), F32R)
        for i in range(22):
            self.act[i].ap = self.act_t[:, i, :]
        self.wg = [Buf(cx.sb("wg%d" % i, (128, NCH, 256), F32R)) for i in range(2)]
        self.wu = [Buf(cx.sb("wu%d" % i, (128, NCH, 256), F32R)) for i in range(2)]
        self.wd = [Buf(cx.sb("wd%d" % i, (128, 512), F32R)) for i in range(3)]
        self.sg = [Buf(cx.sb("sg%d" % i, (128, TT), F32)) for i in range(2)]
        self.rstd = Buf(cx.sb("rstd", (128, TT), F32))
        self.tmp = [Buf(cx.sb("ftmp%d" % i, (128, TT), F32)) for i in range(2)]
        self.hre = [Buf(cx.sb("hre%d" % i, (128, TT), F32)) for i in range(2)]
        self.ho = [Buf(cx.sb("ho%d" % i, (128, TT), F32)) for i in range(2)]

    def run(self, h_src, h_src_buf, h_dst, h_dst_buf, wgate, wup, wdown, gpre, gpost_half, gbuf,
            final=False, h_in_sbuf=False):
        cx, cm = self.cx, self.cm
        PS = cm.PS
        if not h_in_sbuf:
            cx.dma("pool", self.hy.ap[:], h_src.rearrange("c p t -> p c t"), reads=[h_src_buf], writes=[self.hy])
        rms_rstd(cm, self.hy.ap, self.hy, self.rstd, PS[4])
        for c in range(NCH):
            cx.op("dve", lambda e: e.scalar_tensor_tensor(out=self.xn.ap[:, c, :], in0=self.hy.ap[:, c, :],
                                                          scalar=gpre[:, c:c + 1], in1=self.rstd.ap[:],
                                                          op0=ALU.mult, op1=ALU.mult),
                  reads=[self.hy, self.rstd, gbuf], writes=[self.xn])
        wg_v = wgate.rearrange("(c p) f -> p c f", p=128)
        wu_v = wup.rearrange("(c p) f -> p c f", p=128)
        halves = ((0, 22), (22, NF))
        wdi = 0
        for hi, (f0, f1) in enumerate(halves):
            groups = [(f, min(f + 2, f1)) for f in range(f0, f1, 2)]

            def load(gi):
                a, b = groups[gi]
                w = (b - a) * 128
                cx.dma("sp", self.wg[gi % 2].ap[:, :, :w], r_(wg_v[:, :, a * 128:b * 128]), writes=[self.wg[gi % 2]])
                cx.dma("sp", self.wu[gi % 2].ap[:, :, :w], r_(wu_v[:, :, a * 128:b * 128]), writes=[self.wu[gi % 2]])

            load(0)
            for gi, (a, b) in enumerate(groups):
                if gi + 1 < len(groups):
                    load(gi + 1)
                for f in range(a, b):
                    j = f - a
                    pg, pu = PS[(f % 2) * 2], PS[(f % 2) * 2 + 1]
                    for c in range(NCH):
                        cx.op("pe", lambda e: e.matmul(pg.ap[:], self.wg[gi % 2].ap[:, c, j * 128:(j + 1) * 128],
                                                       self.xn.ap[:, c, :], start=(c == 0), stop=(c == NCH - 1)),
                              reads=[self.wg[gi % 2], self.xn], writes=[pg], signal=(c == NCH - 1))
                    for c in range(NCH):
                        cx.op("pe", lambda e: e.matmul(pu.ap[:], self.wu[gi % 2].ap[:, c, j * 128:(j + 1) * 128],
                                                       self.xn.ap[:, c, :], start=(c == 0), stop=(c == NCH - 1)),
                              reads=[self.wu[gi % 2], self.xn], writes=[pu], signal=(c == NCH - 1))
                    sg = self.sg[f % 2]
                    cx.op("act", lambda e: e.activation(out=sg.ap[:], in_=pg.ap[:], func=AF.Silu),
                          reads=[pg], writes=[sg])
                    ab = self.act[f - f0]
                    cx.op("dve", lambda e: e.tensor_tensor(out=ab.ap, in0=sg.ap[:], in1=pu.ap[:], op=ALU.mult),
                          reads=[sg, pu], writes=[ab])
            for dq in range(4):
                ys = [PS[4 + k] for k in range(4)]

                def loadd(f):
                    nonlocal wdi
                    b = self.wd[wdi % 3]
                    wdi += 1
                    cx.dma("sp", b.ap[:], r_(wdown[f * 128:(f + 1) * 128, dq * 512:(dq + 1) * 512]), writes=[b])
                    return b

                pendb = [loadd(f0)]
                if f0 + 1 < f1:
                    pendb.append(loadd(f0 + 1))
                for f in range(f0, f1):
                    if f + 2 < f1:
                        pendb.append(loadd(f + 2))
                    wb = pendb.pop(0)
                    for k in range(4):
                        cx.op("pe", lambda e: e.matmul(ys[k].ap[:], wb.ap[:, k * 128:(k + 1) * 128],
                                                       self.act[f - f0].ap, start=(f == f0), stop=(f == f1 - 1)),
                              reads=[wb, self.act[f - f0]], writes=[ys[k]], signal=(k == 3))
                for k in range(4):
                    c = dq * 4 + k
                    if hi == 0:
                        cx.op("act", lambda e: e.copy(out=self.hy.ap[:, c, :], in_=ys[k].ap[:]),
                              reads=[ys[k]], writes=[self.hy])
                    else:
                        cx.op("dve", lambda e: e.tensor_tensor(out=self.hy.ap[:, c, :], in0=self.hy.ap[:, c, :],
                                                               in1=ys[k].ap[:], op=ALU.add),
                              reads=[ys[k], self.hy], writes=[self.hy])
        self.post(h_src, h_src_buf, h_dst, h_dst_buf, gpost_half, gbuf, final=final)

    def post(self, h_src, h_src_buf, h_dst, h_dst_buf, gpost, gbuf, final=False):
        cx, cm = self.cx, self.cm
        rms_rstd(cm, self.hy.ap, self.hy, self.rstd, cm.PS[4])
        for c in range(NCH):
            hre = self.hre[c % 2]
            cx.dma("pool", hre.ap[:], h_src[c], reads=[h_src_buf], writes=[hre])
            t = self.tmp[c % 2]
            cx.op("dve", lambda e: e.scalar_tensor_tensor(out=t.ap[:], in0=self.hy.ap[:, c, :],
                                                          scalar=gpost[:, c:c + 1], in1=self.rstd.ap[:],
                                                          op0=ALU.mult, op1=ALU.mult),
                  reads=[self.hy, self.rstd, gbuf], writes=[t])
            cx.op("pool", lambda e: e.tensor_tensor(out=self.hy.ap[:, c, :], in0=t.ap[:], in1=hre.ap[:], op=ALU.add),
                  reads=[t, hre], writes=[self.hy])
            cx.dma("pool", h_dst[c], self.hy.ap[:, c, :], reads=[self.hy], writes=[h_dst_buf], final=final)


def prenorm(ffn, gpre, gbuf):
    cx, cm = ffn.cx, ffn.cm
    rms_rstd(cm, ffn.hy.ap, ffn.hy, ffn.rstd, cm.PS[4])
    for c in range(NCH):
        cx.op("dve", lambda e: e.scalar_tensor_tensor(out=ffn.xn.ap[:, c, :], in0=ffn.hy.ap[:, c, :],
                                                      scalar=gpre[:, c:c + 1], in1=ffn.rstd.ap[:],
                                                      op0=ALU.mult, op1=ALU.mult),
              reads=[ffn.hy, ffn.rstd, gbuf], writes=[ffn.xn])


def proj(ffn, winf, winv, pt_dst, pv_dst, ptbuf, pvbuf):
    cx, cm = ffn.cx, ffn.cm
    PS = cm.PS
    wbufs = [ffn.wg[0], ffn.wu[0], ffn.wg[1], ffn.wu[1]]
    stg = ffn.sg + ffn.tmp + ffn.ho
    wv = winf.rearrange("(c p) f -> p c f", p=128)
    ng = (NFM + 1) // 2
    gate0 = FM["gn"][0]

    def load(gi):
        a, b = 2 * gi, min(2 * gi + 2, NFM)
        cx.dma("sp", wbufs[gi % 4].ap[:, :, :(b - a) * 128], r_(wv[:, :, a * 128:b * 128]), writes=[wbufs[gi % 4]])

    load(0)
    load(1)
    k = 0
    for gi in range(ng):
        if gi + 2 < ng:
            load(gi + 2)
        wb = wbufs[gi % 4]
        for f in range(2 * gi, min(2 * gi + 2, NFM)):
            j = f - 2 * gi
            ps = PS[k % 4]
            for c in range(NCH):
                cx.op("pe", lambda e: e.matmul(ps.ap[:], wb.ap[:, c, j * 128:(j + 1) * 128], ffn.xn.ap[:, c, :],
                                               start=(c == 0), stop=(c == NCH - 1)),
                      reads=[wb, ffn.xn], writes=[ps], signal=(c == NCH - 1))
            st = stg[k % len(stg)]
            k += 1
            fn = AF.Sigmoid if f >= gate0 else AF.Copy
            cx.op("act", lambda e: e.activation(out=st.ap[:], in_=ps.ap[:], func=fn), reads=[ps], writes=[st])
            cx.dma("pool", pt_dst[f], st.ap[:], reads=[st], writes=[ptbuf])
    wv2 = winv.rearrange("(c p) f -> p c f", p=128)
    nvg = NVM // 256

    def loadv(gi):
        cx.dma("sp", wbufs[gi % 4].ap[:], r_(wv2[:, :, gi * 256:(gi + 1) * 256]), writes=[wbufs[gi % 4]])

    loadv(0)
    loadv(1)
    for gi in range(nvg):
        if gi + 2 < nvg:
            loadv(gi + 2)
        wb = wbufs[gi % 4]
        for sub in range(TT // 128):
            ps = PS[k % 4]
            for c in range(NCH):
                cx.op("pe", lambda e: e.matmul(ps.ap[:, :256], ffn.xn.ap[:, c, sub * 128:(sub + 1) * 128], wb.ap[:, c, :],
                                               start=(c == 0), stop=(c == NCH - 1)),
                      reads=[wb, ffn.xn], writes=[ps], signal=(c == NCH - 1))
            st = stg[k % len(stg)]
            k += 1
            cx.op("act", lambda e: e.activation(out=st.ap[:, :256], in_=ps.ap[:, :256], func=AF.Copy),
                  reads=[ps], writes=[st])
            cx.dma("pool", pv_dst[sub * 128:(sub + 1) * 128, gi * 256:(gi + 1) * 256], st.ap[:, :256],
                   reads=[st], writes=[pvbuf])


def merge_out(ffn, yT, ytbuf, gates, gtbuf, wup, wout, ):
    cx, cm = ffn.cx, ffn.cm
    PS = cm.PS
    ybufs = ffn.act[:12]
    for i in range(12):
        cx.dma("sp", ybufs[i].ap, r_(yT[i]), reads=[ytbuf], writes=[ybufs[i]])
    wbufs = [ffn.wg[0], ffn.wu[0], ffn.wg[1], ffn.wu[1]]
    gt = ffn.sg + ffn.tmp + ffn.ho + ffn.hre
    wv = wup.rearrange("(c p) f -> p c f", p=128)
    segs = ((0, 6), (6, 8), (8, 12))

    def load(gi):
        cx.dma("sp", wbufs[gi % 4].ap[:, :12, :], r_(wv[:, :, gi * 256:(gi + 1) * 256]), writes=[wbufs[gi % 4]])

    load(0)
    load(1)
    gk = 0
    for gi in range(NCH // 2):
        if gi + 2 < NCH // 2:
            load(gi + 2)
        wb = wbufs[gi % 4]
        for c in range(2 * gi, 2 * gi + 2):
            j = c - 2 * gi
            gts = []
            for b in range(3):
                g = gt[gk % 8]
                gk += 1
                cx.dma("pool", g.ap[:], gates[b * NCH + c], reads=[gtbuf], writes=[g])
                gts.append(g)
            for b, (k0, k1) in enumerate(segs):
                ps = PS[b]
                for kk in range(k0, k1):
                    cx.op("pe", lambda e: e.matmul(ps.ap[:], wb.ap[:, kk, j * 128:(j + 1) * 128], ybufs[kk].ap,
                                                   start=(kk == k0), stop=(kk == k1 - 1)),
                          reads=[wb, ybufs[kk]], writes=[ps], signal=(kk == k1 - 1))
            cx.op("dve", lambda e: e.tensor_tensor(out=gts[0].ap[:], in0=gts[0].ap[:], in1=PS[0].ap[:], op=ALU.mult),
                  reads=[PS[0], gts[0]], writes=[gts[0]])
            cx.op("dve", lambda e: e.tensor_tensor(out=gts[1].ap[:], in0=gts[1].ap[:], in1=PS[1].ap[:], op=ALU.mult),
                  reads=[PS[1], gts[1]], writes=[gts[1]])
            cx.op("dve", lambda e: e.tensor_tensor(out=gts[2].ap[:], in0=gts[2].ap[:], in1=PS[2].ap[:], op=ALU.mult),
                  reads=[PS[2], gts[2]], writes=[gts[2]])
            cx.op("pool", lambda e: e.tensor_tensor(out=gts[0].ap[:], in0=gts[0].ap[:], in1=gts[1].ap[:], op=ALU.add),
                  reads=[gts[1], gts[0]], writes=[gts[0]])
            cx.op("dve", lambda e: e.tensor_tensor(out=ffn.xn.ap[:, c, :], in0=gts[0].ap[:], in1=gts[2].ap[:], op=ALU.add),
                  reads=[gts[2], gts[0]], writes=[ffn.xn])
    wv = wout.rearrange("(c p) f -> p c f", p=128)

    def load2(gi):
        cx.dma("sp", wbufs[gi % 4].ap[:], r_(wv[:, :, gi * 256:(gi + 1) * 256]), writes=[wbufs[gi % 4]])

    load2(0)
    load2(1)
    for gi in range(NCH // 2):
        if gi + 2 < NCH // 2:
            load2(gi + 2)
        wb = wbufs[gi % 4]
        for c in range(2 * gi, 2 * gi + 2):
            j = c - 2 * gi
            ps = PS[4 + c % 4]
            for kk in range(NCH):
                cx.op("pe", lambda e: e.matmul(ps.ap[:], wb.ap[:, kk, j * 128:(j + 1) * 128], ffn.xn.ap[:, kk, :],
                                               start=(kk == 0), stop=(kk == NCH - 1)),
                      reads=[wb, ffn.xn], writes=[ps], signal=(kk == NCH - 1))
            cx.op("act", lambda e: e.copy(out=ffn.hy.ap[:, c, :], in_=ps.ap[:]), reads=[ps], writes=[ffn.hy])


N_ALIBI = 18
_sl = 2.0 ** (-8.0 * np.arange(1, N_ALIBI + 1, dtype=np.float64) / N_ALIBI)
_idx = np.arange(N_ALIBI)
_nsa_idx = _idx[::3][:6]
SL_NSA = _sl[_nsa_idx]
SL_DIL = _sl[np.setdiff1d(_idx, _nsa_idx)].reshape(3, 4)
DIL = ((128, 1), (512, 4), (2048, 16))
SC_NSA = 128 ** -0.5
SC_DIL = 64 ** -0.5
NEGB = -1.0e4
DIL_M = [w // 128 + 1 for w, _ in DIL]
DIL_OFF = [0, 2, 7]


DILN = [m + 3 for m in DIL_M]
DIL_OFF2 = [0, 5, 13]
NDIL = 33


def attn_tables(S, j):
    NBLK = S // 64
    NI = S // 128
    sh = 3 - j
    p = np.arange(128)[:, None].astype(np.float64)
    tl = np.arange(128)[None, :].astype(np.float64)
    p1 = p[:, 0]
    t = {}
    NU = NI + 3
    cb = np.full((128, NU, 6), NEGB)
    for uu in range(NU):
        u = uu - sh
        if u < 0:
            continue
        ok = (16 * p1 <= 128 * u + 96)
        for h in range(6):
            cb[:, uu, h] = np.where(ok, -SL_NSA[h] * (128 * u + 33 - 16 * p1), NEGB)
    t["cmpb"] = cb.reshape(128, NU * 6)
    cm = np.zeros((128, 20, 128))
    for uu in range(20):
        u = uu - sh
        if u < 0:
            continue
        k = p - 8 * u
        cm[:, uu, :] = np.where(k < -1, 1.0, np.where(k > 6, 0.0, (tl >= 16 * k + 31) * 1.0))
    t["cmpm"] = cm.reshape(128, 20 * 128)
    sb_ = np.full((128, NU, 6), NEGB)
    for mm in range(NU):
        m = mm - sh
        if m < 0:
            continue
        for h in range(6):
            sb_[:, mm, h] = -SL_NSA[h] * (128 * m + 64 - p1)
    t["slcb"] = sb_.reshape(128, NU * 6)
    caus, anti = (p <= tl) * 1.0, (p > tl) * 1.0
    tm = np.zeros((128, 4, 128))
    wm = np.zeros((128, 8, 128))
    for mm in range(8):
        m = mm - sh
        if mm < 4:
            tm[:, mm, :] = 0.0 if m < 0 else (caus if m == 0 else 1.0)
        wm[:, mm, :] = 0.0 if (m < 0 or m > 4) else (caus if m == 0 else (anti if m == 4 else 1.0))
    t["trim"] = tm.reshape(128, 512)
    t["winm"] = wm.reshape(128, 1024)
    x = np.arange(2 * NBLK)[None, :]
    jr = x - NBLK - 2 * j
    cur = (np.arange(128)[:, None] >= 64) * 1
    m2 = np.where(jr > cur, -1e30, np.where(jr >= cur - 1, 1e9, 0.0))
    t["m2"] = m2
    t["m1"] = (m2 == 0) * 1.0
    NCC = (S // 16 + 127) // 128
    W = np.zeros((NCC * 128, NBLK))
    for jb in range(NBLK):
        for o, wgt in ((-1, .5), (0, 1.), (1, 1.), (2, 1.), (3, .5)):
            c = 4 * jb + o
            if 0 <= c < S // 16 - 1:
                W[c, jb] = wgt
    t["wsel"] = W.reshape(NCC, 128, NBLK).transpose(1, 0, 2).reshape(128, NCC * NBLK)
    E = np.zeros((32, 18, 128))
    for k in range(18):
        E[k, k, :] = 1.0
    t["esel"] = E.reshape(32, 18 * 128)
    t["ident"] = np.eye(128)
    db = np.full((128, NDIL, 4), NEGB)
    dk = np.zeros((128, NDIL, 128))
    dr = np.zeros((128, 3, 4, 128))
    for g, (w, d) in enumerate(DIL):
        res = (np.mod(tl - p, d) == 0)
        for h in range(4):
            dr[:, g, h, :] = np.exp(-SL_DIL[g, h] * (tl - 64))
        for mm in range(DILN[g]):
            m = mm - sh
            if m < 0 or m >= DIL_M[g]:
                continue
            for h in range(4):
                db[:, DIL_OFF2[g] + mm, h] = -SL_DIL[g, h] * (128 * m + 64 - p1)
            kk = res
            if m == 0:
                kk = kk & (p <= tl)
            if m == DIL_M[g] - 1:
                kk = kk & (p >= tl)
            dk[:, DIL_OFF2[g] + mm, :] = kk
    t["dilb"] = db.reshape(128, NDIL * 4)
    t["dilk"] = dk.reshape(128, NDIL * 128)
    t["dilr"] = dr.reshape(128, 3 * 512)
    ho = np.zeros((128, 2, 128))
    ho[:, 0, :64] = 1.0
    ho[:, 1, 64:] = 1.0
    t["hones"] = ho.reshape(128, 256)
    return {k: np.ascontiguousarray(v, dtype=np.float32) for k, v in t.items()}


class Attn:
    def __init__(self, cx, cm, S, tabs):
        self.cx, self.cm, self.S = cx, cm, S
        self.tabs = tabs
        self.NBLK = S // 64
        self.NI = S // 128
        self.NQ = S // 512
        self.NCC = (S // 16 + 127) // 128
        self.NCMP = S // 16 - 1
        NI, NBLK, NCC = self.NI, self.NBLK, self.NCC
        sb = cx.sb
        self.kcmpT = Buf(sb("kcmpT", (128, 2, NCC * 128), F32R))
        self.vcmp = Buf(sb("vcmp", (128, NCC, 2, 128), F32R))
        self.mkT = Buf(sb("mkT", (128, 4, 256), F32R))
        self.mv = Buf(sb("mv", (128, 2, 512), F32R))
        self.vd32 = Buf(sb("vd32", (128, 512), F32))
        cx.op("pool", lambda e: e.memset(self.vd32.ap[:], 0.0), writes=[self.vd32])

    def alloc_q(self):
        cx = self.cx
        sb = cx.sb
        NBLK, NCC = self.NBLK, self.NCC
        NI = self.NI
        tabs = self.tabs
        self.T = {}
        shapes = {"cmpb": (128, (NI + 3) * 6), "cmpm": (128, 20 * 128), "slcb": (128, (NI + 3) * 6), "trim": (128, 512),
                  "winm": (128, 1024), "m2": (128, 2 * NBLK), "m1": (128, 2 * NBLK), "dilb": (128, NDIL * 4),
                  "dilk": (128, NDIL * 128), "dilr": (128, 1536)}
        for k, shp in shapes.items():
            b = Buf(sb("T" + k, shp, F32))
            cx.dma("pool", b.ap[:], tabs[k], writes=[b])
            self.T[k] = b
        for k, shp in {"wsel": (128, NCC * NBLK), "esel": (32, 18 * 128), "ident": (128, 128),
                       "hones": (128, 256)}.items():
            b = Buf(sb("T" + k, shp, F32R))
            cx.dma("sp", b.ap[:], r_(tabs[k]), writes=[b])
            self.T[k] = b
        self.qa = Buf(sb("qa", (128, 768), F32R))
        self.qb = Buf(sb("qb", (128, 6, 128), F32R))
        self.qm = Buf(sb("qm", (128, 4, 128), F32R))
        self.qbm = Buf(sb("qbm", (128, 6, 2, 128), F32R))
        for k3 in range(3):
            cx.op("dve", lambda e: e.tensor_copy(out=self.qbm.ap[:].rearrange("p a b q -> p (a b q)")[:, k3 * 512:(k3 + 1) * 512],
                                                 in_=self.vd32.ap[:]), reads=[self.vd32], writes=[self.qbm])
        self.gn = Buf(sb("gn", (32, 128), F32R))
        self.G = Buf(sb("G", (128, 18, 128), F32))
        self.Pc = [Buf(sb("Pc%d" % i, (128, 384), F32R)) for i in range(NCC)]
        self.Ps = [Buf(sb("Ps%d" % i, (128, 512), F32R)) for i in range(3)]
        self.rd = Buf(sb("rd", (128, 512), F32))
        self.fac = Buf(sb("fac", (128, 384), F32))
        self.ya = Buf(sb("ya", (128, 768), F32))
        self.yt = Buf(sb("yt", (128, 384), F32))
        self.yb = Buf(sb("yb", (128, 256), F32))
        self.ym = Buf(sb("ym", (128, 512), F32))
        self.fin = Buf(sb("fin", (128, NBLK), F32))
        self.fin2 = Buf(sb("fin2", (128, NBLK), F32))
        self.mx = Buf(sb("mx", (128, 16), F32))
        self.sel = [Buf(sb("sel%d" % g, (128, NBLK), F32)) for g in range(2)]
        self.selx = [Buf(sb("selx%d" % g, (128, 128), F32R)) for g in range(3)]
        self.vt = [Buf(sb("vt%d" % i, (128, 1, 256), F32R)) for i in range(3)]
        self.kd = [Buf(sb("kd%d" % i, (128, 2, 128), F32R)) for i in range(3)]
        self.vd = [Buf(sb("vd%d" % i, (128, 4, 128), F32R)) for i in range(3)]
        for b in self.vd:
            cx.op("dve", lambda e: e.tensor_copy(out=b.ap[:].rearrange("p a b -> p (a b)"), in_=self.vd32.ap[:]),
                  reads=[self.vd32], writes=[b])
        self.ki = 0
        self.di = 0

    def prep_mem(self, memT, wkv, gmem, gbuf, scr):
        cx, cm = self.cx, self.cm
        PS = cm.PS
        scf = self.scf
        mt = Buf(scf.ap[:, 0:NCH * 256].rearrange("p (c t) -> p c t", c=NCH))
        rs = Buf(scf.ap[:, NCH * 256:NCH * 256 + 256])
        mn = Buf(scr.ap[:, 0:NCH * 256].rearrange("p (c t) -> p c t", c=NCH))
        wb = Buf(scr.ap[:, NCH * 256:NCH * 256 + NCH * 512].rearrange("p (c t) -> p c t", c=NCH))
        cx.dma("pool", mt.ap, memT.rearrange("c p t -> p c t"), writes=[mt, scr])
        rms_rstd(cm, mt.ap, mt, rs, PS[4], ncols=256)
        for c in range(NCH):
            cx.op("dve", lambda e: e.scalar_tensor_tensor(out=mn.ap[:, c, :], in0=mt.ap[:, c, :], scalar=gmem[:, c:c + 1],
                                                          in1=rs.ap, op0=ALU.mult, op1=ALU.mult),
                  reads=[mt, rs, gbuf], writes=[mn])
        wv = wkv.rearrange("(c p) f -> p c f", p=128)
        cx.dma("sp", wb.ap, r_(wv[:, :, 0:512]), writes=[wb])
        for h in range(4):
            ps = PS[h % 2]
            for c in range(NCH):
                cx.op("pe", lambda e: e.matmul(ps.ap[:, :256], wb.ap[:, c, h * 128:(h + 1) * 128], mn.ap[:, c, :],
                                               start=(c == 0), stop=(c == NCH - 1)), reads=[wb, mn], writes=[ps],
                      signal=(c == NCH - 1))
            cx.op("act", lambda e: e.activation(out=self.mkT.ap[:, h, :], in_=ps.ap[:, :256], func=AF.Copy),
                  reads=[ps], writes=[self.mkT])
        cx.dma("sp", wb.ap, r_(wv[:, :, 512:1024]), writes=[wb])
        for mc in range(2):
            ps = PS[2 + mc]
            for c in range(NCH):
                cx.op("pe", lambda e: e.matmul(ps.ap[:], mn.ap[:, c, mc * 128:(mc + 1) * 128], wb.ap[:, c, :],
                                               start=(c == 0), stop=(c == NCH - 1)), reads=[wb, mn], writes=[ps],
                      signal=(c == NCH - 1))
            cx.op("act", lambda e: e.activation(out=self.mv.ap[:, mc, :], in_=ps.ap[:], func=AF.Copy),
                  reads=[ps], writes=[self.mv])

    def prep_cmp(self, KT, ktbuf, kvi, peT2, w1, w2, scr):
        cx, cm = self.cx, self.cm
        PS = cm.PS
        S, NCC = self.S, self.NCC
        NCP = NCC * 128
        kfull = Buf(scr.ap[:, 0:S + 16])
        o = S + 16
        w1b = Buf(scr.ap[:, o:o + 32 * 256].rearrange("p (a b) -> p a b", a=32)); o += 32 * 256
        w2b = Buf(scr.ap[:, o:o + 2 * 128].rearrange("p (a b) -> p a b", a=2)); o += 256
        peb = Buf(scr.ap[:, o:o + 64]); o += 64
        hb = Buf(self.scf.ap[:, 0:2])
        hT = Buf(scr.ap[:, o:o + 2 * NCP].rearrange("p (a b) -> p a b", a=2)); o += 2 * NCP
        cx.dma("sp", w1b.ap, r_(w1.rearrange("(a p) h -> p a h", p=128)), reads=[], writes=[w1b, scr])
        cx.dma("sp", w2b.ap, r_(w2.rearrange("(a p) d -> p a d", p=128)), writes=[w2b])
        cx.dma("sp", peb.ap, r_(peT2), writes=[peb])
        cx.op("dve", lambda e: e.tensor_copy(out=kfull.ap[:, S:S + 16], in_=self.vd32.ap[:, 0:16]),
              reads=[self.vd32], writes=[kfull])
        for hc in range(2):
            ps = PS[hc]
            for pos in range(32):
                cx.op("pe", lambda e: e.matmul(ps.ap[:, 0:2], w1b.ap[:, pos, hc * 128:(hc + 1) * 128],
                                               peb.ap[:, 2 * pos:2 * pos + 2], start=(pos == 0), stop=(pos == 31)),
                      reads=[w1b, peb], writes=[ps], signal=(pos == 31))
            cx.op("act", lambda e: e.activation(out=hb.ap[:, hc:hc + 1], in_=ps.ap[:, 0:1], func=AF.Copy),
                  reads=[ps], writes=[hb])
        for g in range(2):
            ch = FM_K["kc" if kvi == 0 else "vc"] + g
            for rk in range(4):
                for i0 in range(0, self.NQ, 4):
                    dst = kfull.ap[:, 0:S].rearrange("p (i r t) -> p i r t", r=4, t=128)[:, i0:i0 + 4, rk, :]
                    src = KT[rk, ch][:, i0 * 128:(i0 + 4) * 128].rearrange("p (i t) -> p i t", t=128)
                    cx.dma("sp", dst, r_(src), reads=[ktbuf], writes=[kfull])
            for hc in range(2):
                for c0 in range(0, NCP, 512):
                    n = min(512, NCP - c0)
                    ps = PS[2 + (c0 // 512) % 2]
                    for pos in range(32):
                        rhs = kfull.ap[:, 16 * c0 + pos: 16 * c0 + pos + 16 * (n - 1) + 1: 16]
                        cx.op("pe", lambda e: e.matmul(ps.ap[:, :n], w1b.ap[:, pos, hc * 128:(hc + 1) * 128], rhs,
                                                       start=(pos == 0), stop=(pos == 31)),
                              reads=[w1b, kfull], writes=[ps], signal=(pos == 31))
                    cx.op("act", lambda e: e.activation(out=hT.ap[:, hc, c0:c0 + n], in_=ps.ap[:, :n], func=AF.Silu,
                                                        bias=hb.ap[:, hc:hc + 1]), reads=[ps, hb], writes=[hT])
            if kvi == 0:
                for c0 in range(0, NCP, 512):
                    n = min(512, NCP - c0)
                    ps = PS[4 + (c0 // 512) % 2]
                    for hc in range(2):
                        cx.op("pe", lambda e: e.matmul(ps.ap[:, :n], w2b.ap[:, hc, :], hT.ap[:, hc, c0:c0 + n],
                                                       start=(hc == 0), stop=(hc == 1)), reads=[w2b, hT], writes=[ps],
                              signal=(hc == 1))
                    cx.op("act", lambda e: e.activation(out=self.kcmpT.ap[:, g, c0:c0 + n], in_=ps.ap[:, :n], func=AF.Copy),
                          reads=[ps], writes=[self.kcmpT])
            else:
                for cc in range(NCC):
                    ps = PS[4 + cc % 2]
                    for hc in range(2):
                        cx.op("pe", lambda e: e.matmul(ps.ap[:, :128], hT.ap[:, hc, cc * 128:cc * 128 + 128], w2b.ap[:, hc, :],
                                                       start=(hc == 0), stop=(hc == 1)), reads=[w2b, hT], writes=[ps],
                              signal=(hc == 1))
                    cx.op("act", lambda e: e.activation(out=self.vcmp.ap[:, cc, g, :], in_=ps.ap[:, :128], func=AF.Copy),
                          reads=[ps], writes=[self.vcmp])

    def _b3(self, ap2):
        return ap2.unsqueeze(1).to_broadcast([128, 3, 128])

    def _v3(self, ap2):
        return ap2.rearrange("p (r q) -> p r q", r=3)

    def _branch_out(self, g, b, den, ops, first):
        cx = self.cx
        Gv = self.G.ap[:, 9 * g + b: 9 * g + b + 7: 3, :]
        dst = self._v3(self.ya.ap[:, 384 * g:384 * (g + 1)])
        if den is None:
            facv = Gv
            facb = self.G
        else:
            cx.op("dve", lambda e: e.tensor_scalar_max(out=self.rd.ap[:, :384], in0=den.ap[:, :384], scalar1=1e-36),
                  reads=[den], writes=[self.rd])
            cx.op("dve", lambda e: e.reciprocal(out=self.rd.ap[:, :384], in_=self.rd.ap[:, :384]),
                  reads=[self.rd], writes=[self.rd])
            cx.op("dve", lambda e: e.tensor_tensor(out=self._v3(self.fac.ap[:]), in0=self._v3(self.rd.ap[:, :384]), in1=Gv,
                                                   op=ALU.mult), reads=[self.rd, self.G], writes=[self.fac])
            facv = self._v3(self.fac.ap[:])
            facb = self.fac
        if first:
            cx.op("dve", lambda e: e.tensor_tensor(out=dst, in0=self._v3(ops.ap[:, :384]), in1=facv, op=ALU.mult),
                  reads=[ops, facb], writes=[self.ya])
        else:
            cx.op("dve", lambda e: e.tensor_tensor(out=self._v3(self.yt.ap[:]), in0=self._v3(ops.ap[:, :384]), in1=facv,
                                                   op=ALU.mult), reads=[ops, facb], writes=[self.yt])
            cx.op("pool", lambda e: e.tensor_tensor(out=self.ya.ap[:, 384 * g:384 * (g + 1)], in0=self.ya.ap[:, 384 * g:384 * (g + 1)],
                                                    in1=self.yt.ap[:], op=ALU.add), reads=[self.yt, self.ya], writes=[self.ya])

    def qblock(self, i, PT, ptbuf, KT, ktbuf, VA, vabuf, yT, ytbuf):
        cx, cm, T = self.cx, self.cm, self.T
        PS = cm.PS
        NBLK = self.NBLK
        I3 = 4 * i + 3
        cx.mute = False
        cs = slice(i * 128, (i + 1) * 128)
        ones = cm.ones

        def fmload(buf, dst, name, n=None):
            c0, k = FM[name]
            k = n or k
            cx.dma("sp", dst, r_(PT[c0:c0 + k, :, cs].rearrange("c p t -> p c t")), reads=[ptbuf], writes=[buf])

        fmload(self.qa, self.qa.ap[:].rearrange("p (c t) -> p c t", c=6), "qa")
        fmload(self.qb, self.qb.ap[:], "qb")
        fmload(self.qm, self.qm.ap[:], "qm")
        cx.op("dve", lambda e: e.tensor_copy(out=self.qbm.ap[0:64, :, 0, :], in_=self.qb.ap[0:64, :, :]),
              reads=[self.qb], writes=[self.qbm])
        cx.op("dve", lambda e: e.tensor_copy(out=self.qbm.ap[64:128, :, 1, :], in_=self.qb.ap[64:128, :, :]),
              reads=[self.qb], writes=[self.qbm])
        cx.dma("sp", self.gn.ap[:], r_(PT[FM["gn"][0], 0:32, cs]), reads=[ptbuf], writes=[self.gn])
        cx.mute = "gates" not in PARTS
        for k in range(18):
            ps = PS[k // 4]
            cx.op("pe", lambda e: e.matmul(ps.ap[:, (k % 4) * 128:(k % 4 + 1) * 128], T["esel"].ap[:, k * 128:(k + 1) * 128],
                                           self.gn.ap[:], start=True, stop=True), reads=[T["esel"], self.gn], writes=[ps],
                  signal=(k % 4 == 3 or k == 17))
        for b in range(5):
            n = 4 if b < 4 else 2
            cx.op("act", lambda e: e.activation(out=self.G.ap[:, 4 * b:4 * b + n, :].rearrange("p a q -> p (a q)"),
                                                in_=PS[b].ap[:, :n * 128], func=AF.Copy), reads=[PS[b]], writes=[self.G])
        cx.mute = "cmp" not in PARTS
        nch = min(self.NCC, (8 * I3 + 6) // 128 + 1)
        for g in range(2):
            qg = self.qa.ap[:, 384 * g:384 * (g + 1)]
            den, ops, scp = PS[4], PS[5], PS[6]
            cx.mute = "cmp" not in PARTS
            for ch in range(nch):
                sp_ = PS[ch % 2]
                cx.op("pe", lambda e: e.matmul(sp_.ap[:, :384], self.kcmpT.ap[:, g, ch * 128:(ch + 1) * 128], qg,
                                               start=True, stop=True), reads=[self.kcmpT, self.qa], writes=[sp_])
                u = I3 - 16 * ch
                P = self.Pc[ch]
                for r in range(3):
                    col = u * 6 + 3 * g + r
                    cx.op("act", lambda e: e.activation(out=P.ap[:, r * 128:(r + 1) * 128], in_=sp_.ap[:, r * 128:(r + 1) * 128],
                                                        func=AF.Exp, scale=SC_NSA, bias=T["cmpb"].ap[:, col:col + 1]),
                          reads=[sp_, T["cmpb"]], writes=[P])
                if u <= 19:
                    cx.op("dve", lambda e: e.tensor_tensor(out=self._v3(P.ap[:]), in0=self._v3(P.ap[:]),
                                                           in1=self._b3(T["cmpm"].ap[:, u * 128:(u + 1) * 128]), op=ALU.mult),
                          reads=[P, T["cmpm"]], writes=[P])
                cx.op("pe", lambda e: e.matmul(den.ap[:, :384], ones.ap[:], P.ap[:], start=(ch == 0), stop=(ch == nch - 1)),
                      reads=[ones, P], writes=[den])
            cx.op("dve", lambda e: e.tensor_scalar_max(out=self.rd.ap[:, :384], in0=den.ap[:, :384], scalar1=1e-36),
                  reads=[den], writes=[self.rd])
            cx.op("dve", lambda e: e.reciprocal(out=self.rd.ap[:, :384], in_=self.rd.ap[:, :384]),
                  reads=[self.rd], writes=[self.rd])
            for ch in range(nch):
                P = self.Pc[ch]
                cx.op("dve", lambda e: e.tensor_tensor(out=P.ap[:], in0=P.ap[:], in1=self.rd.ap[:, :384], op=ALU.mult),
                      reads=[P, self.rd], writes=[P])
                cx.op("pe", lambda e: e.matmul(ops.ap[:, :384], self.vcmp.ap[:, ch, g, :], P.ap[:], start=(ch == 0),
                                               stop=(ch == nch - 1)), reads=[self.vcmp, P], writes=[ops])
                for r in range(3):
                    cx.op("pe", lambda e: e.matmul(scp.ap[:, :NBLK], P.ap[:, r * 128:(r + 1) * 128],
                                                   T["wsel"].ap[:, ch * NBLK:(ch + 1) * NBLK],
                                                   start=(ch == 0 and r == 0), stop=(ch == nch - 1 and r == 2)),
                          reads=[T["wsel"], P], writes=[scp])
            cx.mute = "cmp" not in PARTS
            self._branch_out(g, 0, None, ops, True)
            cx.mute = "sel" not in PARTS
            x0 = NBLK - 8 * i
            fin, fin2, mx = self.fin, self.fin2, self.mx
            cx.op("dve", lambda e: e.tensor_tensor(out=fin.ap[:], in0=scp.ap[:, :NBLK], in1=T["m1"].ap[:, x0:x0 + NBLK], op=ALU.mult),
                  reads=[scp, T["m1"]], writes=[fin])
            cx.op("dve", lambda e: e.tensor_tensor(out=fin.ap[:], in0=fin.ap[:], in1=T["m2"].ap[:, x0:x0 + NBLK], op=ALU.add),
                  reads=[fin, T["m2"]], writes=[fin])
            cx.op("dve", lambda e: e.memset(fin.ap[:, 0:1], 1.0e9), reads=[fin], writes=[fin])
            cx.op("dve", lambda e: e.max(out=mx.ap[:, 0:8], in_=fin.ap[:]), reads=[fin], writes=[mx])
            cx.op("dve", lambda e: e.match_replace(out=fin2.ap[:], in_to_replace=mx.ap[:, 0:8], in_values=fin.ap[:],
                                                   imm_value=-3.0e38), reads=[fin, mx], writes=[fin2])
            cx.op("dve", lambda e: e.max(out=mx.ap[:, 8:16], in_=fin2.ap[:]), reads=[fin2, mx], writes=[mx])
            cx.op("dve", lambda e: e.tensor_scalar(out=self.sel[g].ap[:], in0=fin.ap[:], scalar1=mx.ap[:, 15:16], scalar2=None,
                                                   op0=ALU.is_ge), reads=[fin, mx], writes=[self.sel[g]])
        cx.mute = False
        for br, (kname, vcol, k_lo) in enumerate((("ks", 0, 0), ("kw", 256, max(0, I3 - 7)))):
            dens = (PS[4], PS[5])
            opss = (PS[6], PS[7])
            cx.mute = ("slc", "win")[br] not in PARTS
            for kc in range(k_lo, I3 + 1):
                rk, loc = kc % 4, kc // 4
                m = I3 - kc
                kt, vt = self.kd[self.ki % 3], self.vt[self.ki % 3]
                self.ki += 1
                k0 = FM_K[kname]
                cx.dma("sp", kt.ap[:], r_(KT[rk, k0:k0 + 2, :, loc * 128:(loc + 1) * 128].rearrange("g p t -> p g t")),
                       reads=[ktbuf], writes=[kt])
                cx.dma("sp", vt.ap[:, 0, :], r_(VA[rk, loc * 128:(loc + 1) * 128, vcol:vcol + 256]), reads=[vabuf], writes=[vt])
                for g in range(2):
                    qg = self.qa.ap[:, 384 * g:384 * (g + 1)]
                    sp_ = PS[g]
                    cx.op("pe", lambda e: e.matmul(sp_.ap[:, :384], kt.ap[:, g, :], qg, start=True, stop=True),
                          reads=[kt, self.qa], writes=[sp_])
                    if br == 0:
                        mp_ = PS[2 + g]
                        sx = self.selx[(2 * kc + g) % 3]
                        cx.op("dve", lambda e: e.tensor_copy(out=sx.ap[:].rearrange("p (a b) -> p a b", a=2),
                                                             in_=self.sel[g].ap[:, 2 * kc:2 * kc + 2].unsqueeze(2).to_broadcast([128, 2, 64])),
                              reads=[self.sel[g]], writes=[sx])
                        cx.op("pe", lambda e: e.matmul(mp_.ap[:, :128], sx.ap[:], T["ident"].ap[:], start=True, stop=True),
                              reads=[sx, T["ident"]], writes=[mp_])
                    P = self.Ps[(2 * kc + g) % 3]
                    for r in range(3):
                        col = m * 6 + 3 * g + r
                        cx.op("act", lambda e: e.activation(out=P.ap[:, r * 128:(r + 1) * 128], in_=sp_.ap[:, r * 128:(r + 1) * 128],
                                                            func=AF.Exp, scale=SC_NSA, bias=T["slcb"].ap[:, col:col + 1]),
                              reads=[sp_, T["slcb"]], writes=[P])
                    Pv = self._v3(P.ap[:, :384])
                    if br == 0:
                        cx.op("dve", lambda e: e.tensor_tensor(out=Pv, in0=Pv, in1=self._b3(mp_.ap[:, :128]), op=ALU.mult),
                              reads=[P, mp_], writes=[P])
                    if br == 0 and m <= 3:
                        cx.op("dve", lambda e: e.tensor_tensor(out=Pv, in0=Pv, in1=self._b3(T["trim"].ap[:, m * 128:(m + 1) * 128]), op=ALU.mult),
                              reads=[P, T["trim"]], writes=[P])
                    if br == 1:
                        cx.op("dve", lambda e: e.tensor_tensor(out=Pv, in0=Pv, in1=self._b3(T["winm"].ap[:, m * 128:(m + 1) * 128]), op=ALU.mult),
                              reads=[P, T["winm"]], writes=[P])
                    cx.op("pe", lambda e: e.matmul(dens[g].ap[:, :384], ones.ap[:], P.ap[:, :384], start=(kc == k_lo), stop=(kc == I3)),
                          reads=[ones, P], writes=[dens[g]])
                    cx.op("pe", lambda e: e.matmul(opss[g].ap[:, :384], vt.ap[:, 0, g * 128:(g + 1) * 128], P.ap[:, :384],
                                                   start=(kc == k_lo), stop=(kc == I3)), reads=[vt, P], writes=[opss[g]])
            for g in range(2):
                self._branch_out(g, 1 + br, dens[g], opss[g], False)
        cx.mute = False
        cx.dma("pool", yT[0:6, :, cs].rearrange("c p t -> p c t"), self.ya.ap[:].rearrange("p (c t) -> p c t", c=6),
               reads=[self.ya], writes=[ytbuf])
        cx.mute = "dil" not in PARTS
        obs, dbs = (PS[4], PS[5]), (PS[6], PS[7])
        items = [(g, m) for g in range(3) for m in range(DILN[g]) if I3 - m >= 0]
        for ii, (g, m) in enumerate(items):
            kc = I3 - m
            rk, loc = kc % 4, kc // 4
            kt, vt = self.kd[self.ki % 3], self.vd[self.di % 3]
            self.ki += 1
            self.di += 1
            k0 = FM_K["kb"] + 2 * g
            cx.dma("sp", kt.ap[:], r_(KT[rk, k0:k0 + 2, :, loc * 128:(loc + 1) * 128].rearrange("g p t -> p g t")),
                   reads=[ktbuf], writes=[kt])
            for par in range(2):
                src = VA[rk, loc * 128:(loc + 1) * 128, 512 + g * 256 + par * 64:512 + g * 256 + 256].rearrange(
                    "k (a b) -> k a b", b=128)[:, :, 0:64] if par == 0 else \
                    VA[rk, loc * 128:(loc + 1) * 128, 512 + g * 256:512 + g * 256 + 256].rearrange(
                    "k (a b) -> k a b", b=128)[:, :, 64:128]
                dst = vt.ap[:, par::2, par * 64:par * 64 + 64]
                cx.dma("sp", dst, r_(src), reads=[vabuf], writes=[vt])
            sp_ = PS[ii % 2]
            for h in range(4):
                cx.op("pe", lambda e: e.matmul(sp_.ap[:, h * 128:(h + 1) * 128], kt.ap[:, h // 2, :],
                                               self.qbm.ap[:, 2 * g + h // 2, h % 2, :], start=True, stop=True),
                      reads=[kt, self.qbm], writes=[sp_], signal=(h == 3))
            P = self.Ps[ii % 3]
            for h in range(4):
                col = (DIL_OFF2[g] + m) * 4 + h
                cx.op("act", lambda e: e.activation(out=P.ap[:, h * 128:(h + 1) * 128], in_=sp_.ap[:, h * 128:(h + 1) * 128],
                                                    func=AF.Exp, scale=SC_DIL, bias=T["dilb"].ap[:, col:col + 1]),
                      reads=[sp_, T["dilb"]], writes=[P])
            kd_ = DIL_OFF2[g] + m
            P4 = P.ap[:].rearrange("p (h q) -> p h q", h=4)
            cx.op("dve", lambda e: e.tensor_tensor(out=P4, in0=P4, in1=T["dilk"].ap[:, kd_ * 128:(kd_ + 1) * 128].unsqueeze(1).to_broadcast([128, 4, 128]),
                                                   op=ALU.mult), reads=[P, T["dilk"]], writes=[P])
            cx.op("dve", lambda e: e.tensor_tensor(out=P.ap[:], in0=P.ap[:], in1=T["dilr"].ap[:, g * 512:(g + 1) * 512], op=ALU.mult),
                  reads=[P, T["dilr"]], writes=[P])
            first, last = (ii == 0), (ii == len(items) - 1)
            for h in range(4):
                pr = h // 2
                cx.op("pe", lambda e: e.matmul(obs[pr].ap[:, 0:128], vt.ap[:, h, :], P.ap[:, h * 128:(h + 1) * 128],
                                               start=(first and h % 2 == 0), stop=(last and h % 2 == 1)),
                      reads=[vt, P], writes=[obs[pr]], signal=False)
                cx.op("pe", lambda e: e.matmul(dbs[pr].ap[:, 0:128], T["hones"].ap[:, (h % 2) * 128:(h % 2 + 1) * 128],
                                               P.ap[:, h * 128:(h + 1) * 128], start=(first and h % 2 == 0), stop=(last and h % 2 == 1)),
                      reads=[T["hones"], P], writes=[dbs[pr]], signal=(h == 3))
        for pr in range(2):
            cx.op("dve", lambda e: e.reciprocal(out=self.rd.ap[:, pr * 128:(pr + 1) * 128], in_=dbs[pr].ap[:, :128]),
                  reads=[dbs[pr]], writes=[self.rd])
            cx.op("dve", lambda e: e.tensor_tensor(out=self.yb.ap[:, pr * 128:(pr + 1) * 128], in0=obs[pr].ap[:, :128],
                                                   in1=self.rd.ap[:, pr * 128:(pr + 1) * 128], op=ALU.mult),
                  reads=[obs[pr], self.rd], writes=[self.yb])
        cx.dma("pool", yT[6:8, :, cs].rearrange("c p t -> p c t"), self.yb.ap[:].rearrange("p (c t) -> p c t", c=2),
               reads=[self.yb], writes=[ytbuf])
        cx.mute = "mem" not in PARTS
        om, dm = PS[6], PS[7]
        for mc in range(2):
            sp_ = PS[mc]
            for h in range(4):
                cx.op("pe", lambda e: e.matmul(sp_.ap[:, h * 128:(h + 1) * 128], self.mkT.ap[:, h, mc * 128:(mc + 1) * 128],
                                               self.qm.ap[:, h, :], start=True, stop=True),
                      reads=[self.mkT, self.qm], writes=[sp_], signal=(h == 3))
            P = self.Ps[mc]
            cx.op("act", lambda e: e.activation(out=P.ap[:], in_=sp_.ap[:], func=AF.Exp, scale=SC_NSA), reads=[sp_], writes=[P])
        for h in range(4):
            for mc in range(2):
                P = self.Ps[mc]
                cx.op("pe", lambda e: e.matmul(om.ap[:, h * 128:(h + 1) * 128], self.mv.ap[:, mc, h * 128:(h + 1) * 128],
                                               P.ap[:, h * 128:(h + 1) * 128], start=(mc == 0), stop=(mc == 1)),
                      reads=[self.mv, P], writes=[om], signal=(mc == 1))
        for h in range(4):
            for mc in range(2):
                P = self.Ps[mc]
                cx.op("pe", lambda e: e.matmul(dm.ap[:, h * 128:(h + 1) * 128], ones.ap[:], P.ap[:, h * 128:(h + 1) * 128],
                                               start=(mc == 0), stop=(mc == 1)), reads=[ones, P], writes=[dm], signal=(mc == 1))
        cx.op("dve", lambda e: e.reciprocal(out=self.rd.ap[:], in_=dm.ap[:]), reads=[dm], writes=[self.rd])
        cx.op("dve", lambda e: e.tensor_tensor(out=self.ym.ap[:], in0=om.ap[:], in1=self.rd.ap[:], op=ALU.mult),
              reads=[om, self.rd], writes=[self.ym])
        cx.dma("pool", yT[8:12, :, cs].rearrange("c p t -> p c t"), self.ym.ap[:].rearrange("p (c t) -> p c t", c=4),
               reads=[self.ym], writes=[ytbuf])
        cx.mute = False


def _dt(nc, name, shape, kind):
    return nc.dram_tensor(name, list(shape), F32, kind=kind).ap()


def build_AC(S, do_C, do_A, last):
    NT = S // 4
    nc = bass.Bass("TRN2", target_bir_lowering=False)
    nc.dge_precook = False
    I_, O_ = "ExternalInput", "ExternalOutput"
    h_in = _dt(nc, "h_in", (NCH, 128, NT), I_)
    gv = _dt(nc, "gv", (128, 6 * NCH), I_)
    if do_C:
        yT = _dt(nc, "yT", (12, 128, NT), I_)
        gates = _dt(nc, "gates", (48, 128, NT), I_)
        wup = _dt(nc, "wup", (1536, D), I_)
        wout = _dt(nc, "wout", (D, D), I_)
        f2g, f2u, f2d = _dt(nc, "f2g", (D, DFF), I_), _dt(nc, "f2u", (D, DFF), I_), _dt(nc, "f2d", (DFF, D), I_)
        h2 = _dt(nc, "h2s", (NCH, 128, NT), "Internal")
    if do_A:
        f1g, f1u, f1d = _dt(nc, "f1g", (D, DFF), I_), _dt(nc, "f1u", (D, DFF), I_), _dt(nc, "f1d", (DFF, D), I_)
        winf = _dt(nc, "winf", (D, NFM * 128), I_)
        winv = _dt(nc, "winv", (D, NVM), I_)
        h1o = _dt(nc, "h1o", (NCH, 128, NT), O_)
        pto = _dt(nc, "pto", (NFM, 128, NT), O_)
        pvo = _dt(nc, "pvo", (NT, NVM), O_)
    if do_C:
        h3 = _dt(nc, "h3o", (NCH, 128, NT), O_ if not do_A else "Internal")
    with ExitStack() as es:
        cx = Ctx(nc, es)
        cm = Common(cx, None)
        g = Buf(cx.sb("g", (128, 6 * NCH), F32))
        cx.dma("sp", g.ap[:], gv, writes=[g])
        for k in (2, 4):
            cx.op("dve", lambda e: e.tensor_single_scalar(out=g.ap[:, k * NCH:(k + 1) * NCH], in_=g.ap[:, k * NCH:(k + 1) * NCH],
                                                          scalar=0.5, op=ALU.mult), reads=[g], writes=[g])
        G = lambda k: g.ap[:, k * NCH:(k + 1) * NCH]
        ffn = FFN(cx, cm)
        bin_, b2, b3, b1, bpt, bpv, by, bg = (Buf() for _ in range(8))
        for t in range(NT // TT):
            ts = slice(t * TT, (t + 1) * TT)
            src, srcb = h_in[:, :, ts], bin_
            insb = False
            if do_C:
                merge_out(ffn, yT[:, :, ts], by, gates[:, :, ts], bg, wup, wout)
                ffn.post(src, srcb, h2[:, :, ts], b2, G(0), g)
                ffn.run(h2[:, :, ts], b2, h3[:, :, ts], b3, f2g, f2u, f2d, G(1), G(2), g, final=not do_A, h_in_sbuf=True)
                src, srcb, insb = h3[:, :, ts], b3, True
            if do_A:
                ffn.run(src, srcb, h1o[:, :, ts], b1, f1g, f1u, f1d, G(3), G(4), g, final=True, h_in_sbuf=insb)
                prenorm(ffn, G(5), g)
                proj(ffn, winf, winv, pto[:, :, ts], pvo[ts, :], bpt, bpv)
        cx.finish()
    return nc


TAB_SHAPES = None


def build_B(S):
    NT = S // 4
    nc = bass.Bass("TRN2", target_bir_lowering=False)
    nc.dge_precook = False
    I_, O_ = "ExternalInput", "ExternalOutput"
    PT = _dt(nc, "PT", (NFM, 128, NT), I_)
    KT = _dt(nc, "KT", (4, 14, 128, NT), I_)
    VA = _dt(nc, "VA", (4, NT, NVM), I_)
    memT = _dt(nc, "memT", (NCH, 128, 256), I_)
    wkv = _dt(nc, "wkv", (D, 1024), I_)
    gm = _dt(nc, "gmem", (128, NCH), I_)
    pek, pev = _dt(nc, "pek", (128, 64), I_), _dt(nc, "pev", (128, 64), I_)
    kw1, kw2 = _dt(nc, "kw1", (4096, 256), I_), _dt(nc, "kw2", (256, 128), I_)
    vw1, vw2 = _dt(nc, "vw1", (4096, 256), I_), _dt(nc, "vw2", (256, 128), I_)
    tabs_np = attn_tables(S, 0)
    tabs = {k: _dt(nc, "t_" + k, v.shape, I_) for k, v in tabs_np.items()}
    yT = _dt(nc, "yT", (12, 128, NT), O_)
    with ExitStack() as es:
        cx = Ctx(nc, es)
        cm = Common(cx, None)
        at = Attn(cx, cm, S, tabs)
        g = Buf(cx.sb("gmem_sb", (128, NCH), F32))
        cx.dma("sp", g.ap[:], gm, writes=[g])
        bpt, bkt, bva, byt = (Buf() for _ in range(4))
        with ExitStack() as es2:
            nscr = max(NCH * 256 + NCH * 512, S + 16 + 32 * 256 + 256 + 64 + 2 * at.NCC * 128)
            scr = Buf(es2.enter_context(nc.sbuf_tensor("scr", [128, nscr], F32R)))
            at.scf = Buf(es2.enter_context(nc.sbuf_tensor("scf", [128, NCH * 256 + 256], F32)))
            if "mem0" in PARTS:
                at.prep_mem(memT, wkv, g.ap, g, scr)
            cx.barrier()
            if "cmp0" in PARTS:
                at.prep_cmp(KT, bkt, 0, pek, kw1, kw2, scr)
            cx.barrier()
            if "cmp0" in PARTS:
                at.prep_cmp(KT, bkt, 1, pev, vw1, vw2, scr)
            cx.barrier()
        at.alloc_q()
        for i in range(at.NQ):
            at.qblock(i, PT, bpt, KT, bkt, VA, bva, yT, byt)
        cx.finish()
    return nc


IN_NAMES = ("q_a", "kc", "vc", "ks", "vs", "kw", "vw", "g_nsa", "q_b", "k_b", "v_b", "q_m", "g_a", "g_b", "g_m")
IN_SIZES = (768, 256, 256, 256, 256, 256, 256, 18, 768, 768, 768, 512, 2048, 2048, 2048)
_PROGS = {}
_DBG = {}
PARTS = {"mem0", "cmp0", "gates", "cmp", "sel", "slc", "win", "dil", "mem"}


def _prog(key, fn):
    if key not in _PROGS:
        _PROGS[key] = fn()
    return _PROGS[key]


def _fm(a2d):
    return np.ascontiguousarray(a2d.T).reshape(NCH, 128, -1)


def _gt(vec):
    return vec.reshape(NCH, 128).T


def _split_win(w):
    offs = np.cumsum((0,) + IN_SIZES)
    col = {n: w[:, offs[i]:offs[i + 1]] for i, n in enumerate(IN_NAMES)}
    gn = np.zeros((D, 128), np.float32)
    gn[:, :18] = col["g_nsa"]
    winf = np.concatenate([col["q_a"], col["kc"], col["vc"], col["ks"], col["kw"], col["q_b"], col["k_b"], col["q_m"],
                           gn, col["g_a"], col["g_b"], col["g_m"]], axis=1)
    winv = np.concatenate([col["vs"], col["vw"], col["v_b"]], axis=1)
    return np.ascontiguousarray(winf), np.ascontiguousarray(winv)


def _run(nc, maps):
    res = run_bass_kernel_spmd(nc, maps, core_ids=list(range(8)))
    return res.results


def kernel(**inp):
    x = np.asarray(inp["x"], np.float32)
    B, S, _ = x.shape
    NT, NI = S // 4, S // 128
    L = inp["w_in"].shape[0]
    P = {k: np.asarray(v, np.float32) for k, v in inp.items()}

    def shard(xb, j):
        return xb.reshape(NI // 4, 4, 128, D)[:, j].reshape(NT, D)

    h = [_fm(shard(x[c // 4], c % 4)) for c in range(8)]
    memT = [np.ascontiguousarray(P["mem"][b].T).reshape(NCH, 128, 256) for b in range(B)]
    tabs = [attn_tables(S, j) for j in range(4)]
    wins = [_split_win(P["w_in"][l]) for l in range(L)]

    def gv(lc, la):
        cols = []
        for nm, l in (("mix_post_g", lc), ("ffn2_pre_g", lc), ("ffn2_post_g", lc), ("ffn1_pre_g", la), ("ffn1_post_g", la),
                      ("mix_pre_g", la)):
            cols.append(_gt(P[nm][min(max(l, 0), L - 1)]))
        return np.ascontiguousarray(np.concatenate(cols, axis=1))

    def a_inputs(l):
        return {"f1g": P["ffn1_w_gate"][l], "f1u": P["ffn1_w_up"][l], "f1d": P["ffn1_w_down"][l],
                "winf": wins[l][0], "winv": wins[l][1]}

    def c_inputs(l, yT, pto):
        g0 = FM["ga"][0]
        return [{"yT": yT[c], "gates": np.ascontiguousarray(pto[c][g0:g0 + 48]),
                 "wup": np.ascontiguousarray(np.concatenate([P["w_up_nsa"][l], P["w_up_dil"][l], P["w_up_mem"][l]], axis=0)),
                 "wout": P["w_out"][l], "f2g": P["ffn2_w_gate"][l], "f2u": P["ffn2_w_up"][l], "f2d": P["ffn2_w_down"][l]}
                for c in range(8)]

    ncA = _prog(("AC", S, False, True), lambda: build_AC(S, False, True, False))
    base = a_inputs(0)
    g_ = gv(0, 0)
    res = _run(ncA, [dict(base, h_in=h[c], gv=g_) for c in range(8)])
    for l in range(L):
        h1 = [r["h1o"] for r in res]
        _DBG["h1_%d" % l] = h1
        pto = [r["pto"] for r in res]
        pvo = [r["pvo"] for r in res]
        ksel = np.concatenate([np.arange(FM[n][0], FM[n][0] + FM[n][1]) for n in KSIDE])
        KT = [np.ascontiguousarray(np.stack([pto[4 * b + j][ksel] for j in range(4)])) for b in range(B)]
        VA = [np.ascontiguousarray(np.stack([pvo[4 * b + j] for j in range(4)])) for b in range(B)]
        ncB = _prog(("B", S), lambda: build_B(S))
        pe2 = lambda a: np.ascontiguousarray(np.repeat(a.T, 2, axis=1))
        maps = []
        for c in range(8):
            b, j = c // 4, c % 4
            m = {"PT": pto[c], "KT": KT[b], "VA": VA[b], "memT": memT[b], "wkv": P["w_mem_kv"][l],
                 "gmem": np.ascontiguousarray(_gt(P["mem_norm_g"][l])), "pek": pe2(P["cmp_pe_k"][l]), "pev": pe2(P["cmp_pe_v"][l]),
                 "kw1": P["cmp_k_w1"][l], "kw2": P["cmp_k_w2"][l], "vw1": P["cmp_v_w1"][l], "vw2": P["cmp_v_w2"][l]}
            for k, v in tabs[j].items():
                m["t_" + k] = v
            maps.append(m)
        resB = _run(ncB, maps)
        yT = [r["yT"] for r in resB]
        _DBG["yT_%d" % l] = yT
        _DBG["pto_%d" % l] = pto
        last = (l == L - 1)
        ncC = _prog(("AC", S, True, not last), lambda: build_AC(S, True, not last, last))
        cin = c_inputs(l, yT, pto)
        g_ = gv(l, l + 1)
        extra = {} if last else a_inputs(l + 1)
        res = _run(ncC, [dict(cin[c], h_in=h1[c], gv=g_, **extra) for c in range(8)])
    out = np.zeros((B, S, D), np.float32)
    for c in range(8):
        o = res[c]["h3o"].reshape(D, NT).T.reshape(NI // 4, 128, D)
        out[c // 4].reshape(NI // 4, 4, 128, D)[:, c % 4] = o
    return out
```

```python
import numpy as np
import ml_dtypes
from contextlib import ExitStack
import concourse.bass as bass
import concourse.mybir as mybir
from concourse.bass_utils import run_bass_kernel_spmd

F32 = mybir.dt.float32
F32R = mybir.dt.float32r
BF16 = mybir.dt.bfloat16
AF = mybir.ActivationFunctionType
ALU = mybir.AluOpType

D = 2048
NCH = 16
DFF = 5504
NF = 43
EPS = 1e-6
TT = 512
NDS = 8

FM = {}
_o = 0
for _n, _k in (("qa", 6), ("kc", 2), ("vc", 2), ("ks", 2), ("kw", 2), ("qb", 6), ("kb", 6),
               ("qm", 4), ("gn", 1), ("ga", 16), ("gb", 16), ("gm", 16)):
    FM[_n] = (_o, _k)
    _o += _k
NFM = _o
NVM = 1280
KSIDE = ("kc", "vc", "ks", "kw", "kb")
FM_K = {"kc": 0, "vc": 2, "ks": 4, "kw": 6, "kb": 8}


class Buf:
    __slots__ = ("w", "r", "ap")

    def __init__(self, ap=None):
        self.w = None
        self.r = {}
        self.ap = ap


class Ctx:
    def __init__(self, nc, es):
        self.nc = nc
        self.es = es
        self.engs = {"pe": nc.tensor, "dve": nc.vector, "act": nc.scalar, "pool": nc.gpsimd, "sp": nc.sync}
        self.esem = {k: es.enter_context(nc.semaphore("s_" + k)) for k in self.engs}
        self.ecnt = {k: 0 for k in self.engs}
        self.dq = {q: [[es.enter_context(nc.semaphore("d_%s%d" % (q, i))), 0] for i in range(NDS)]
                   for q in ("sp", "pool")}
        self.dqi = {q: 0 for q in self.dq}
        self.seen = {}
        self.pend = {k: ([], []) for k in self.engs}
        self.outs = []
        self.mute = False

    def sb(self, name, shape, dt=F32):
        t = self.es.enter_context(self.nc.sbuf_tensor(name, list(shape), dt))
        return t

    def ps(self, name, shape=(128, 512)):
        return self.es.enter_context(self.nc.psum_tensor(name, list(shape), F32))

    def _wait(self, eng, evs):
        best = {}
        for s, v in evs:
            if best.get(s, 0) < v:
                best[s] = v
        for s, v in best.items():
            k = (eng, s)
            if self.seen.get(k, 0) < v:
                self.engs[eng].wait_ge(s, v)
                self.seen[k] = v

    def _deps(self, reads, writes):
        evs = []
        for b in reads:
            if b.w is not None:
                evs.append(b.w)
        for b in writes:
            if b.w is not None:
                evs.append(b.w)
            evs.extend(b.r.items())
        return evs

    def op(self, eng, fn, reads=(), writes=(), signal=True):
        if self.mute:
            return None
        self._wait(eng, self._deps(reads, writes))
        ins = fn(self.engs[eng])
        pr, pw = self.pend[eng]
        pr.extend(reads)
        pw.extend(writes)
        if signal:
            self.ecnt[eng] += 1
            s = self.esem[eng]
            v = self.ecnt[eng]
            ins.then_inc(s, 1)
            for b in pr:
                b.r[s] = v
            for b in pw:
                b.w = (s, v)
                b.r = {}
            pr.clear()
            pw.clear()
        return ins

    def dma(self, q, out, in_, reads=(), writes=(), final=False):
        if self.mute:
            return None
        evs = self._deps(reads, writes)
        slot = self.dq[q][self.dqi[q] % NDS]
        self.dqi[q] += 1
        if slot[1] > 0:
            evs.append((slot[0], slot[1]))
        self._wait(q, evs)
        ins = self.engs[q].dma_start(out=out, in_=in_)
        slot[1] += 16
        ins.then_inc(slot[0], 16)
        for b in reads:
            b.r[slot[0]] = slot[1]
        for b in writes:
            b.w = (slot[0], slot[1])
            b.r = {}
        if final:
            self.outs.append((slot[0], slot[1]))

    def barrier(self):
        evs = [(self.esem[k], self.ecnt[k]) for k in self.engs if self.ecnt[k] > 0]
        for q in self.dq:
            for sl in self.dq[q]:
                if sl[1] > 0:
                    evs.append((sl[0], sl[1]))
        for k in self.engs:
            self._wait(k, evs)

    def finish(self):
        self.barrier()


def r_(ap):
    return ap.bitcast(F32R)


class Common:
    def __init__(self, cx, consts_ap):
        nc = cx.nc
        self.cx = cx
        self.ones = Buf(cx.sb("ones", (128, 128), F32R))
        self.epsc = Buf(cx.sb("epsc", (128, 1), F32))
        self.ones32 = Buf(cx.sb("ones32", (128, 128), F32))
        cx.op("pool", lambda e: e.memset(self.ones32.ap[:], 1.0), writes=[self.ones32])
        cx.op("dve", lambda e: e.tensor_copy(out=self.ones.ap[:], in_=self.ones32.ap[:]),
              reads=[self.ones32], writes=[self.ones])
        cx.op("pool", lambda e: e.memset(self.epsc.ap[:], EPS), writes=[self.epsc])
        self.PS = [Buf(cx.ps("ps%d" % i)) for i in range(8)]
        self.sq = [Buf(cx.sb("sq%d" % i, (128, TT), F32R)) for i in range(2)]
        self.rt = Buf(cx.sb("rt", (128, TT), F32))


def rms_rstd(cm, src, srcbuf, rstd, psb, ncols=TT, nch=NCH):
    cx = cm.cx
    for c in range(nch):
        sq = cm.sq[c % 2]
        cx.op("act", lambda e: e.activation(out=sq.ap[:, :ncols], in_=src[:, c, :], func=AF.Square),
              reads=[srcbuf], writes=[sq])
        cx.op("pe", lambda e: e.matmul(psb.ap[:, :ncols], cm.ones.ap[:], sq.ap[:, :ncols],
                                       start=(c == 0), stop=(c == nch - 1)),
              reads=[sq, cm.ones], writes=[psb])
    cx.op("act", lambda e: e.activation(out=cm.rt.ap[:, :ncols], in_=psb.ap[:, :ncols], func=AF.Sqrt,
                                        scale=1.0 / (nch * 128), bias=cm.epsc.ap[:]),
          reads=[psb, cm.epsc], writes=[cm.rt])
    cx.op("dve", lambda e: e.reciprocal(out=rstd.ap[:, :ncols], in_=cm.rt.ap[:, :ncols]),
          reads=[cm.rt], writes=[rstd])


class FFN:
    def __init__(self, cx, cm):
        self.cx = cx
        self.cm = cm
        self.xn = Buf(cx.sb("xn", (128, NCH, TT), F32R))
        self.hy = Buf(cx.sb("hy", (128, NCH, TT), F32))
        self.act = [Buf() for _ in range(22)]
        self.act_t = cx.sb("actT", (128, 22, TT), F32R)
        for i in range(22):
            self.act[i].ap = self.act_t[:, i, :]
        self.wg = [Buf(cx.sb("wg%d" % i, (128, NCH, 256), F32R)) for i in range(2)]
        self.wu = [Buf(cx.sb("wu%d" % i, (128, NCH, 256), F32R)) for i in range(2)]
        self.wd = [Buf(cx.sb("wd%d" % i, (128, 512), F32R)) for i in range(3)]
        self.sg = [Buf(cx.sb("sg%d" % i, (128, TT), F32)) for i in range(2)]
        self.rstd = Buf(cx.sb("rstd", (128, TT), F32))
        self.tmp = [Buf(cx.sb("ftmp%d" % i, (128, TT), F32)) for i in range(2)]
        self.hre = [Buf(cx.sb("hre%d" % i, (128, TT), F32)) for i in range(2)]
        self.ho = [Buf(cx.sb("ho%d" % i, (128, TT), F32)) for i in range(2)]

    def run(self, h_src, h_src_buf, h_dst, h_dst_buf, wgate, wup, wdown, gpre, gpost_half, gbuf,
            final=False, h_in_sbuf=False):
        cx, cm = self.cx, self.cm
        PS = cm.PS
        if not h_in_sbuf:
            cx.dma("pool", self.hy.ap[:], h_src.rearrange("c p t -> p c t"), reads=[h_src_buf], writes=[self.hy])
        rms_rstd(cm, self.hy.ap, self.hy, self.rstd, PS[4])
        for c in range(NCH):
            cx.op("dve", lambda e: e.scalar_tensor_tensor(out=self.xn.ap[:, c, :], in0=self.hy.ap[:, c, :],
                                                          scalar=gpre[:, c:c + 1], in1=self.rstd.ap[:],
                                                          op0=ALU.mult, op1=ALU.mult),
                  reads=[self.hy, self.rstd, gbuf], writes=[self.xn])
        wg_v = wgate.rearrange("(c p) f -> p c f", p=128)
        wu_v = wup.rearrange("(c p) f -> p c f", p=128)
        halves = ((0, 22), (22, NF))
        wdi = 0
        for hi, (f0, f1) in enumerate(halves):
            groups = [(f, min(f + 2, f1)) for f in range(f0, f1, 2)]

            def load(gi):
                a, b = groups[gi]
                w = (b - a) * 128
                cx.dma("sp", self.wg[gi % 2].ap[:, :, :w], r_(wg_v[:, :, a * 128:b * 128]), writes=[self.wg[gi % 2]])
                cx.dma("sp", self.wu[gi % 2].ap[:, :, :w], r_(wu_v[:, :, a * 128:b * 128]), writes=[self.wu[gi % 2]])

            load(0)
            for gi, (a, b) in enumerate(groups):
                if gi + 1 < len(groups):
                    load(gi + 1)
                for f in range(a, b):
                    j = f - a
                    pg, pu = PS[(f % 2) * 2], PS[(f % 2) * 2 + 1]
                    for c in range(NCH):
                        cx.op("pe", lambda e: e.matmul(pg.ap[:], self.wg[gi % 2].ap[:, c, j * 128:(j + 1) * 128],
                                                       self.xn.ap[:, c, :], start=(c == 0), stop=(c == NCH - 1)),
                              reads=[self.wg[gi % 2], self.xn], writes=[pg], signal=(c == NCH - 1))
                    for c in range(NCH):
                        cx.op("pe", lambda e: e.matmul(pu.ap[:], self.wu[gi % 2].ap[:, c, j * 128:(j + 1) * 128],
                                                       self.xn.ap[:, c, :], start=(c == 0), stop=(c == NCH - 1)),
                              reads=[self.wu[gi % 2], self.xn], writes=[pu], signal=(c == NCH - 1))
                    sg = self.sg[f % 2]
                    cx.op("act", lambda e: e.activation(out=sg.ap[:], in_=pg.ap[:], func=AF.Silu),
                          reads=[pg], writes=[sg])
                    ab = self.act[f - f0]
                    cx.op("dve", lambda e: e.tensor_tensor(out=ab.ap, in0=sg.ap[:], in1=pu.ap[:], op=ALU.mult),
                          reads=[sg, pu], writes=[ab])
            for dq in range(4):
                ys = [PS[4 + k] for k in range(4)]

                def loadd(f):
                    nonlocal wdi
                    b = self.wd[wdi % 3]
                    wdi += 1
                    cx.dma("sp", b.ap[:], r_(wdown[f * 128:(f + 1) * 128, dq * 512:(dq + 1) * 512]), writes=[b])
                    return b

                pendb = [loadd(f0)]
                if f0 + 1 < f1:
                    pendb.append(loadd(f0 + 1))
                for f in range(f0, f1):
                    if f + 2 < f1:
                        pendb.append(loadd(f + 2))
                    wb = pendb.pop(0)
                    for k in range(4):
                        cx.op("pe", lambda e: e.matmul(ys[k].ap[:], wb.ap[:, k * 128:(k + 1) * 128],
                                                       self.act[f - f0].ap, start=(f == f0), stop=(f == f1 - 1)),
                              reads=[wb, self.act[f - f0]], writes=[ys[k]], signal=(k == 3))
                for k in range(4):
                    c = dq * 4 + k
                    if hi == 0:
                        cx.op("act", lambda e: e.copy(out=self.hy.ap[:, c, :], in_=ys[k].ap[:]),
                              reads=[ys[k]], writes=[self.hy])
                    else:
                        cx.op("dve", lambda e: e.tensor_tensor(out=self.hy.ap[:, c, :], in0=self.hy.ap[:, c, :],
                                                               in1=ys[k].ap[:], op=ALU.add),
                              reads=[ys[k], self.hy], writes=[self.hy])
        self.post(h_src, h_src_buf, h_dst, h_dst_buf, gpost_half, gbuf, final=final)

    def post(self, h_src, h_src_buf, h_dst, h_dst_buf, gpost, gbuf, final=False):
        cx, cm = self.cx, self.cm
        rms_rstd(cm, self.hy.ap, self.hy, self.rstd, cm.PS[4])
        for c in range(NCH):
            hre = self.hre[c % 2]
            cx.dma("pool", hre.ap[:], h_src[c], reads=[h_src_buf], writes=[hre])
            t = self.tmp[c % 2]
            cx.op("dve", lambda e: e.scalar_tensor_tensor(out=t.ap[:], in0=self.hy.ap[:, c, :],
                                                          scalar=gpost[:, c:c + 1], in1=self.rstd.ap[:],
                                                          op0=ALU.mult, op1=ALU.mult),
                  reads=[self.hy, self.rstd, gbuf], writes=[t])
            cx.op("pool", lambda e: e.tensor_tensor(out=self.hy.ap[:, c, :], in0=t.ap[:], in1=hre.ap[:], op=ALU.add),
                  reads=[t, hre], writes=[self.hy])
            cx.dma("pool", h_dst[c], self.hy.ap[:, c, :], reads=[self.hy], writes=[h_dst_buf], final=final)


def prenorm(ffn, gpre, gbuf):
    cx, cm = ffn.cx, ffn.cm
    rms_rstd(cm, ffn.hy.ap, ffn.hy, ffn.rstd, cm.PS[4])
    for c in range(NCH):
        cx.op("dve", lambda e: e.scalar_tensor_tensor(out=ffn.xn.ap[:, c, :], in0=ffn.hy.ap[:, c, :],
                                                      scalar=gpre[:, c:c + 1], in1=ffn.rstd.ap[:],
                                                      op0=ALU.mult, op1=ALU.mult),
              reads=[ffn.hy, ffn.rstd, gbuf], writes=[ffn.xn])


def proj(ffn, winf, winv, pt_dst, pv_dst, ptbuf, pvbuf):
    cx, cm = ffn.cx, ffn.cm
    PS = cm.PS
    wbufs = [ffn.wg[0], ffn.wu[0], ffn.wg[1], ffn.wu[1]]
    stg = ffn.sg + ffn.tmp + ffn.ho
    wv = winf.rearrange("(c p) f -> p c f", p=128)
    ng = (NFM + 1) // 2
    gate0 = FM["gn"][0]

    def load(gi):
        a, b = 2 * gi, min(2 * gi + 2, NFM)
        cx.dma("sp", wbufs[gi % 4].ap[:, :, :(b - a) * 128], r_(wv[:, :, a * 128:b * 128]), writes=[wbufs[gi % 4]])

    load(0)
    load(1)
    k = 0
    for gi in range(ng):
        if gi + 2 < ng:
            load(gi + 2)
        wb = wbufs[gi % 4]
        for f in range(2 * gi, min(2 * gi + 2, NFM)):
            j = f - 2 * gi
            ps = PS[k % 4]
            for c in range(NCH):
                cx.op("pe", lambda e: e.matmul(ps.ap[:], wb.ap[:, c, j * 128:(j + 1) * 128], ffn.xn.ap[:, c, :],
                                               start=(c == 0), stop=(c == NCH - 1)),
                      reads=[wb, ffn.xn], writes=[ps], signal=(c == NCH - 1))
            st = stg[k % len(stg)]
            k += 1
            fn = AF.Sigmoid if f >= gate0 else AF.Copy
            cx.op("act", lambda e: e.activation(out=st.ap[:], in_=ps.ap[:], func=fn), reads=[ps], writes=[st])
            cx.dma("pool", pt_dst[f], st.ap[:], reads=[st], writes=[ptbuf])
    wv2 = winv.rearrange("(c p) f -> p c f", p=128)
    nvg = NVM // 256

    def loadv(gi):
        cx.dma("sp", wbufs[gi % 4].ap[:], r_(wv2[:, :, gi * 256:(gi + 1) * 256]), writes=[wbufs[gi % 4]])

    loadv(0)
    loadv(1)
    for gi in range(nvg):
        if gi + 2 < nvg:
            loadv(gi + 2)
        wb = wbufs[gi % 4]
        for sub in range(TT // 128):
            ps = PS[k % 4]
            for c in range(NCH):
                cx.op("pe", lambda e: e.matmul(ps.ap[:, :256], ffn.xn.ap[:, c, sub * 128:(sub + 1) * 128], wb.ap[:, c, :],
                                               start=(c == 0), stop=(c == NCH - 1)),
                      reads=[wb, ffn.xn], writes=[ps], signal=(c == NCH - 1))
            st = stg[k % len(stg)]
            k += 1
            cx.op("act", lambda e: e.activation(out=st.ap[:, :256], in_=ps.ap[:, :256], func=AF.Copy),
                  reads=[ps], writes=[st])
            cx.dma("pool", pv_dst[sub * 128:(sub + 1) * 128, gi * 256:(gi + 1) * 256], st.ap[:, :256],
                   reads=[st], writes=[pvbuf])


def merge_out(ffn, yT, ytbuf, gates, gtbuf, wup, wout, ):
    cx, cm = ffn.cx, ffn.cm
    PS = cm.PS
    ybufs = ffn.act[:12]
    for i in range(12):
        cx.dma("sp", ybufs[i].ap, r_(yT[i]), reads=[ytbuf], writes=[ybufs[i]])
    wbufs = [ffn.wg[0], ffn.wu[0], ffn.wg[1], ffn.wu[1]]
    gt = ffn.sg + ffn.tmp + ffn.ho + ffn.hre
    wv = wup.rearrange("(c p) f -> p c f", p=128)
    segs = ((0, 6), (6, 8), (8, 12))

    def load(gi):
        cx.dma("sp", wbufs[gi % 4].ap[:, :12, :], r_(wv[:, :, gi * 256:(gi + 1) * 256]), writes=[wbufs[gi % 4]])

    load(0)
    load(1)
    gk = 0
    for gi in range(NCH // 2):
        if gi + 2 < NCH // 2:
            load(gi + 2)
        wb = wbufs[gi % 4]
        for c in range(2 * gi, 2 * gi + 2):
            j = c - 2 * gi
            gts = []
            for b in range(3):
                g = gt[gk % 8]
                gk += 1
                cx.dma("pool", g.ap[:], gates[b * NCH + c], reads=[gtbuf], writes=[g])
                gts.append(g)
            for b, (k0, k1) in enumerate(segs):
                ps = PS[b]
                for kk in range(k0, k1):
                    cx.op("pe", lambda e: e.matmul(ps.ap[:], wb.ap[:, kk, j * 128:(j + 1) * 128], ybufs[kk].ap,
                                                   start=(kk == k0), stop=(kk == k1 - 1)),
                          reads=[wb, ybufs[kk]], writes=[ps], signal=(kk == k1 - 1))
            cx.op("dve", lambda e: e.tensor_tensor(out=gts[0].ap[:], in0=gts[0].ap[:], in1=PS[0].ap[:], op=ALU.mult),
                  reads=[PS[0], gts[0]], writes=[gts[0]])
            cx.op("dve", lambda e: e.tensor_tensor(out=gts[1].ap[:], in0=gts[1].ap[:], in1=PS[1].ap[:], op=ALU.mult),
                  reads=[PS[1], gts[1]], writes=[gts[1]])
            cx.op("dve", lambda e: e.tensor_tensor(out=gts[2].ap[:], in0=gts[2].ap[:], in1=PS[2].ap[:], op=ALU.mult),
                  reads=[PS[2], gts[2]], writes=[gts[2]])
            cx.op("pool", lambda e: e.tensor_tensor(out=gts[0].ap[:], in0=gts[0].ap[:], in1=gts[1].ap[:], op=ALU.add),
                  reads=[gts[1], gts[0]], writes=[gts[0]])
            cx.op("dve", lambda e: e.tensor_tensor(out=ffn.xn.ap[:, c, :], in0=gts[0].ap[:], in1=gts[2].ap[:], op=ALU.add),
                  reads=[gts[2], gts[0]], writes=[ffn.xn])
    wv = wout.rearrange("(c p) f -> p c f", p=128)

    def load2(gi):
        cx.dma("sp", wbufs[gi % 4].ap[:], r_(wv[:, :, gi * 256:(gi + 1) * 256]), writes=[wbufs[gi % 4]])

    load2(0)
    load2(1)
    for gi in range(NCH // 2):
        if gi + 2 < NCH // 2:
            load2(gi + 2)
        wb = wbufs[gi % 4]
        for c in range(2 * gi, 2 * gi + 2):
            j = c - 2 * gi
            ps = PS[4 + c % 4]
            for kk in range(NCH):
                cx.op("pe", lambda e: e.matmul(ps.ap[:], wb.ap[:, kk, j * 128:(j + 1) * 128], ffn.xn.ap[:, kk, :],
                                               start=(kk == 0), stop=(kk == NCH - 1)),
                      reads=[wb, ffn.xn], writes=[ps], signal=(kk == NCH - 1))
            cx.op("act", lambda e: e.copy(out=ffn.hy.ap[:, c, :], in_=ps.ap[:]), reads=[ps], writes=[ffn.hy])


N_ALIBI = 18
_sl = 2.0 ** (-8.0 * np.arange(1, N_ALIBI + 1, dtype=np.float64) / N_ALIBI)
_idx = np.arange(N_ALIBI)
_nsa_idx = _idx[::3][:6]
SL_NSA = _sl[_nsa_idx]
SL_DIL = _sl[np.setdiff1d(_idx, _nsa_idx)].reshape(3, 4)
DIL = ((128, 1), (512, 4), (2048, 16))
SC_NSA = 128 ** -0.5
SC_DIL = 64 ** -0.5
NEGB = -1.0e4
DIL_M = [w // 128 + 1 for w, _ in DIL]
DIL_OFF = [0, 2, 7]


DILN = [m + 3 for m in DIL_M]
DIL_OFF2 = [0, 5, 13]
NDIL = 33


def attn_tables(S, j):
    NBLK = S // 64
    NI = S // 128
    sh = 3 - j
    p = np.arange(128)[:, None].astype(np.float64)
    tl = np.arange(128)[None, :].astype(np.float64)
    p1 = p[:, 0]
    t = {}
    NU = NI + 3
    cb = np.full((128, NU, 6), NEGB)
    for uu in range(NU):
        u = uu - sh
        if u < 0:
            continue
        ok = (16 * p1 <= 128 * u + 96)
        for h in range(6):
            cb[:, uu, h] = np.where(ok, -SL_NSA[h] * (128 * u + 33 - 16 * p1), NEGB)
    t["cmpb"] = cb.reshape(128, NU * 6)
    cm = np.zeros((128, 20, 128))
    for uu in range(20):
        u = uu - sh
        if u < 0:
            continue
        k = p - 8 * u
        cm[:, uu, :] = np.where(k < -1, 1.0, np.where(k > 6, 0.0, (tl >= 16 * k + 31) * 1.0))
    t["cmpm"] = cm.reshape(128, 20 * 128)
    sb_ = np.full((128, NU, 6), NEGB)
    for mm in range(NU):
        m = mm - sh
        if m < 0:
            continue
        for h in range(6):
            sb_[:, mm, h] = -SL_NSA[h] * (128 * m + 64 - p1)
    t["slcb"] = sb_.reshape(128, NU * 6)
    caus, anti = (p <= tl) * 1.0, (p > tl) * 1.0
    tm = np.zeros((128, 4, 128))
    wm = np.zeros((128, 8, 128))
    for mm in range(8):
        m = mm - sh
        if mm < 4:
            tm[:, mm, :] = 0.0 if m < 0 else (caus if m == 0 else 1.0)
        wm[:, mm, :] = 0.0 if (m < 0 or m > 4) else (caus if m == 0 else (anti if m == 4 else 1.0))
    t["trim"] = tm.reshape(128, 512)
    t["winm"] = wm.reshape(128, 1024)
    x = np.arange(2 * NBLK)[None, :]
    jr = x - NBLK - 2 * j
    cur = (np.arange(128)[:, None] >= 64) * 1
    m2 = np.where(jr > cur, -1e30, np.where(jr >= cur - 1, 1e9, 0.0))
    t["m2"] = m2
    t["m1"] = (m2 == 0) * 1.0
    NCC = (S // 16 + 127) // 128
    W = np.zeros((NCC * 128, NBLK))
    for jb in range(NBLK):
        for o, wgt in ((-1, .5), (0, 1.), (1, 1.), (2, 1.), (3, .5)):
            c = 4 * jb + o
            if 0 <= c < S // 16 - 1:
                W[c, jb] = wgt
    t["wsel"] = W.reshape(NCC, 128, NBLK).transpose(1, 0, 2).reshape(128, NCC * NBLK)
    E = np.zeros((32, 18, 128))
    for k in range(18):
        E[k, k, :] = 1.0
    t["esel"] = E.reshape(32, 18 * 128)
    t["ident"] = np.eye(128)
    db = np.full((128, NDIL, 4), NEGB)
    dk = np.zeros((128, NDIL, 128))
    dr = np.zeros((128, 3, 4, 128))
    for g, (w, d) in enumerate(DIL):
        res = (np.mod(tl - p, d) == 0)
        for h in range(4):
            dr[:, g, h, :] = np.exp(-SL_DIL[g, h] * (tl - 64))
        for mm in range(DILN[g]):
            m = mm - sh
            if m < 0 or m >= DIL_M[g]:
                continue
            for h in range(4):
                db[:, DIL_OFF2[g] + mm, h] = -SL_DIL[g, h] * (128 * m + 64 - p1)
            kk = res
            if m == 0:
                kk = kk & (p <= tl)
            if m == DIL_M[g] - 1:
                kk = kk & (p >= tl)
            dk[:, DIL_OFF2[g] + mm, :] = kk
    t["dilb"] = db.reshape(128, NDIL * 4)
    t["dilk"] = dk.reshape(128, NDIL * 128)
    t["dilr"] = dr.reshape(128, 3 * 512)
    ho = np.zeros((128, 2, 128))
    ho[:, 0, :64] = 1.0
    ho[:, 1, 64:] = 1.0
    t["hones"] = ho.reshape(128, 256)
    out = {k: np.ascontiguousarray(v, dtype=np.float32) for k, v in t.items()}
    KB = min(128, NBLK)
    ef = np.zeros((128, 64 * KB), np.float32)
    xx = np.arange(64 * KB)
    for blk in range(KB):
        ef[blk, (xx // 64) == blk] = 1.0
    out["efull"] = ef.astype(ml_dtypes.bfloat16)
    return out


class Attn:
    def __init__(self, cx, cm, S, tabs):
        self.cx, self.cm, self.S = cx, cm, S
        self.tabs = tabs
        self.NBLK = S // 64
        self.NI = S // 128
        self.NQ = S // 512
        self.NCC = (S // 16 + 127) // 128
        self.NCMP = S // 16 - 1
        NI, NBLK, NCC = self.NI, self.NBLK, self.NCC
        sb = cx.sb
        self.kcmpT = Buf(sb("kcmpT", (128, 2, NCC * 128), F32R))
        self.vcmp = Buf(sb("vcmp", (128, NCC, 2, 128), F32R))
        self.mkT = Buf(sb("mkT", (128, 4, 256), F32R))
        self.mv = Buf(sb("mv", (128, 2, 512), F32R))
        self.vd32 = Buf(sb("vd32", (128, 512), F32))
        cx.op("pool", lambda e: e.memset(self.vd32.ap[:], 0.0), writes=[self.vd32])

    def alloc_q(self):
        cx = self.cx
        sb = cx.sb
        NBLK, NCC = self.NBLK, self.NCC
        NI = self.NI
        tabs = self.tabs
        self.T = {}
        shapes = {"cmpb": (128, (NI + 3) * 6), "cmpm": (128, 20 * 128), "slcb": (128, (NI + 3) * 6), "trim": (128, 512),
                  "winm": (128, 1024), "m2": (128, 2 * NBLK), "m1": (128, 2 * NBLK), "dilb": (128, NDIL * 4),
                  "dilk": (128, NDIL * 128), "dilr": (128, 1536)}
        for k, shp in shapes.items():
            b = Buf(sb("T" + k, shp, F32))
            cx.dma("pool", b.ap[:], tabs[k], writes=[b])
            self.T[k] = b
        for k, shp in {"wsel": (128, NCC * NBLK), "esel": (32, 18 * 128), "ident": (128, 128),
                       "hones": (128, 256)}.items():
            b = Buf(sb("T" + k, shp, F32R))
            cx.dma("sp", b.ap[:], r_(tabs[k]), writes=[b])
            self.T[k] = b
        KB = min(128, NBLK)
        self.KB = KB
        b = Buf(sb("Tefull", (128, 64 * KB), BF16))
        cx.dma("pool", b.ap[:], tabs["efull"], writes=[b])
        self.T["efull"] = b
        self.negs = [Buf(sb("negs%d" % g, (128, max(1, NBLK // 128), 384), BF16)) for g in range(2)]
        self.qa = Buf(sb("qa", (128, 768), F32R))
        self.qb = Buf(sb("qb", (128, 6, 128), F32R))
        self.qm = Buf(sb("qm", (128, 4, 128), F32R))
        self.qbm = Buf(sb("qbm", (128, 6, 2, 128), F32R))
        for k3 in range(3):
            cx.op("dve", lambda e: e.tensor_copy(out=self.qbm.ap[:].rearrange("p a b q -> p (a b q)")[:, k3 * 512:(k3 + 1) * 512],
                                                 in_=self.vd32.ap[:]), reads=[self.vd32], writes=[self.qbm])
        self.gn = Buf(sb("gn", (32, 128), F32R))
        self.G = Buf(sb("G", (128, 18, 128), F32))
        self.Pc = [Buf(sb("Pc%d" % i, (128, 384), F32R)) for i in range(NCC)]
        self.Ps = [Buf(sb("Ps%d" % i, (128, 512), F32R)) for i in range(6)]
        self.rd = Buf(sb("rd", (128, 512), F32))
        self.fac = Buf(sb("fac", (128, 384), F32))
        self.ya = Buf(sb("ya", (128, 768), F32))
        self.yt = Buf(sb("yt", (128, 384), F32))
        self.yb = Buf(sb("yb", (128, 256), F32))
        self.ym = Buf(sb("ym", (128, 512), F32))
        self.fin = Buf(sb("fin", (128, NBLK), F32))
        self.fin2 = Buf(sb("fin2", (128, NBLK), F32))
        self.mx = Buf(sb("mx", (128, 16), F32))
        self.sel = [Buf(sb("sel%d" % g, (128, NBLK), F32R)) for g in range(2)]
        self.vt = [Buf(sb("vt%d" % i, (128, 1, 256), F32R)) for i in range(6)]
        self.kd = [Buf(sb("kd%d" % i, (128, 2, 128), F32R)) for i in range(6)]
        self.vd = [Buf(sb("vd%d" % i, (128, 4, 128), F32R)) for i in range(4)]
        for b in self.vd:
            cx.op("dve", lambda e: e.tensor_copy(out=b.ap[:].rearrange("p a b -> p (a b)"), in_=self.vd32.ap[:]),
                  reads=[self.vd32], writes=[b])
        self.ki = 0
        self.di = 0

    def prep_mem(self, memT, wkv, gmem, gbuf, scr):
        cx, cm = self.cx, self.cm
        PS = cm.PS
        scf = self.scf
        mt = Buf(scf.ap[:, 0:NCH * 256].rearrange("p (c t) -> p c t", c=NCH))
        rs = Buf(scf.ap[:, NCH * 256:NCH * 256 + 256])
        mn = Buf(scr.ap[:, 0:NCH * 256].rearrange("p (c t) -> p c t", c=NCH))
        wb = Buf(scr.ap[:, NCH * 256:NCH * 256 + NCH * 512].rearrange("p (c t) -> p c t", c=NCH))
        cx.dma("pool", mt.ap, memT.rearrange("c p t -> p c t"), writes=[mt, scr])
        rms_rstd(cm, mt.ap, mt, rs, PS[4], ncols=256)
        for c in range(NCH):
            cx.op("dve", lambda e: e.scalar_tensor_tensor(out=mn.ap[:, c, :], in0=mt.ap[:, c, :], scalar=gmem[:, c:c + 1],
                                                          in1=rs.ap, op0=ALU.mult, op1=ALU.mult),
                  reads=[mt, rs, gbuf], writes=[mn])
        wv = wkv.rearrange("(c p) f -> p c f", p=128)
        cx.dma("sp", wb.ap, r_(wv[:, :, 0:512]), writes=[wb])
        for h in range(4):
            ps = PS[h % 2]
            for c in range(NCH):
                cx.op("pe", lambda e: e.matmul(ps.ap[:, :256], wb.ap[:, c, h * 128:(h + 1) * 128], mn.ap[:, c, :],
                                               start=(c == 0), stop=(c == NCH - 1)), reads=[wb, mn], writes=[ps],
                      signal=(c == NCH - 1))
            cx.op("act", lambda e: e.activation(out=self.mkT.ap[:, h, :], in_=ps.ap[:, :256], func=AF.Copy),
                  reads=[ps], writes=[self.mkT])
        cx.dma("sp", wb.ap, r_(wv[:, :, 512:1024]), writes=[wb])
        for mc in range(2):
            ps = PS[2 + mc]
            for c in range(NCH):
                cx.op("pe", lambda e: e.matmul(ps.ap[:], mn.ap[:, c, mc * 128:(mc + 1) * 128], wb.ap[:, c, :],
                                               start=(c == 0), stop=(c == NCH - 1)), reads=[wb, mn], writes=[ps],
                      signal=(c == NCH - 1))
            cx.op("act", lambda e: e.activation(out=self.mv.ap[:, mc, :], in_=ps.ap[:], func=AF.Copy),
                  reads=[ps], writes=[self.mv])

    def prep_cmp(self, KT, ktbuf, kvi, peT2, w1, w2, scr):
        cx, cm = self.cx, self.cm
        PS = cm.PS
        S, NCC = self.S, self.NCC
        NCP = NCC * 128
        kfull = Buf(scr.ap[:, 0:S + 16])
        o = S + 16
        w1b = Buf(scr.ap[:, o:o + 32 * 256].rearrange("p (a b) -> p a b", a=32)); o += 32 * 256
        w2b = Buf(scr.ap[:, o:o + 2 * 128].rearrange("p (a b) -> p a b", a=2)); o += 256
        peb = Buf(scr.ap[:, o:o + 64]); o += 64
        hb = Buf(self.scf.ap[:, 0:2])
        hT = Buf(scr.ap[:, o:o + 2 * NCP].rearrange("p (a b) -> p a b", a=2)); o += 2 * NCP
        cx.dma("sp", w1b.ap, r_(w1.rearrange("(a p) h -> p a h", p=128)), reads=[], writes=[w1b, scr])
        cx.dma("sp", w2b.ap, r_(w2.rearrange("(a p) d -> p a d", p=128)), writes=[w2b])
        cx.dma("sp", peb.ap, r_(peT2), writes=[peb])
        cx.op("dve", lambda e: e.tensor_copy(out=kfull.ap[:, S:S + 16], in_=self.vd32.ap[:, 0:16]),
              reads=[self.vd32], writes=[kfull])
        for hc in range(2):
            ps = PS[hc]
            for pos in range(32):
                cx.op("pe", lambda e: e.matmul(ps.ap[:, 0:2], w1b.ap[:, pos, hc * 128:(hc + 1) * 128],
                                               peb.ap[:, 2 * pos:2 * pos + 2], start=(pos == 0), stop=(pos == 31)),
                      reads=[w1b, peb], writes=[ps], signal=(pos == 31))
            cx.op("act", lambda e: e.activation(out=hb.ap[:, hc:hc + 1], in_=ps.ap[:, 0:1], func=AF.Copy),
                  reads=[ps], writes=[hb])
        for g in range(2):
            ch = FM_K["kc" if kvi == 0 else "vc"] + g
            for rk in range(4):
                for i0 in range(0, self.NQ, 4):
                    dst = kfull.ap[:, 0:S].rearrange("p (i r t) -> p i r t", r=4, t=128)[:, i0:i0 + 4, rk, :]
                    src = KT[rk, ch][:, i0 * 128:(i0 + 4) * 128].rearrange("p (i t) -> p i t", t=128)
                    cx.dma("sp", dst, r_(src), reads=[ktbuf], writes=[kfull])
            for hc in range(2):
                for c0 in range(0, NCP, 512):
                    n = min(512, NCP - c0)
                    ps = PS[2 + (c0 // 512) % 2]
                    for pos in range(32):
                        rhs = kfull.ap[:, 16 * c0 + pos: 16 * c0 + pos + 16 * (n - 1) + 1: 16]
                        cx.op("pe", lambda e: e.matmul(ps.ap[:, :n], w1b.ap[:, pos, hc * 128:(hc + 1) * 128], rhs,
                                                       start=(pos == 0), stop=(pos == 31)),
                              reads=[w1b, kfull], writes=[ps], signal=(pos == 31))
                    cx.op("act", lambda e: e.activation(out=hT.ap[:, hc, c0:c0 + n], in_=ps.ap[:, :n], func=AF.Silu,
                                                        bias=hb.ap[:, hc:hc + 1]), reads=[ps, hb], writes=[hT])
            if kvi == 0:
                for c0 in range(0, NCP, 512):
                    n = min(512, NCP - c0)
                    ps = PS[4 + (c0 // 512) % 2]
                    for hc in range(2):
                        cx.op("pe", lambda e: e.matmul(ps.ap[:, :n], w2b.ap[:, hc, :], hT.ap[:, hc, c0:c0 + n],
                                                       start=(hc == 0), stop=(hc == 1)), reads=[w2b, hT], writes=[ps],
                              signal=(hc == 1))
                    cx.op("act", lambda e: e.activation(out=self.kcmpT.ap[:, g, c0:c0 + n], in_=ps.ap[:, :n], func=AF.Copy),
                          reads=[ps], writes=[self.kcmpT])
            else:
                for cc in range(NCC):
                    ps = PS[4 + cc % 2]
                    for hc in range(2):
                        cx.op("pe", lambda e: e.matmul(ps.ap[:, :128], hT.ap[:, hc, cc * 128:cc * 128 + 128], w2b.ap[:, hc, :],
                                                       start=(hc == 0), stop=(hc == 1)), reads=[w2b, hT], writes=[ps],
                              signal=(hc == 1))
                    cx.op("act", lambda e: e.activation(out=self.vcmp.ap[:, cc, g, :], in_=ps.ap[:, :128], func=AF.Copy),
                          reads=[ps], writes=[self.vcmp])

    def _b3(self, ap2):
        return ap2.unsqueeze(1).to_broadcast([128, 3, 128])

    def _v3(self, ap2):
        return ap2.rearrange("p (r q) -> p r q", r=3)

    def _branch_out(self, g, b, den, ops, first):
        cx = self.cx
        Gv = self.G.ap[:, 9 * g + b: 9 * g + b + 7: 3, :]
        dst = self._v3(self.ya.ap[:, 384 * g:384 * (g + 1)])
        if den is None:
            facv = Gv
            facb = self.G
        else:
            cx.op("dve", lambda e: e.tensor_scalar_max(out=self.rd.ap[:, :384], in0=den.ap[:, :384], scalar1=1e-36),
                  reads=[den], writes=[self.rd])
            cx.op("dve", lambda e: e.reciprocal(out=self.rd.ap[:, :384], in_=self.rd.ap[:, :384]),
                  reads=[self.rd], writes=[self.rd])
            cx.op("dve", lambda e: e.tensor_tensor(out=self._v3(self.fac.ap[:]), in0=self._v3(self.rd.ap[:, :384]), in1=Gv,
                                                   op=ALU.mult), reads=[self.rd, self.G], writes=[self.fac])
            facv = self._v3(self.fac.ap[:])
            facb = self.fac
        if first:
            cx.op("dve", lambda e: e.tensor_tensor(out=dst, in0=self._v3(ops.ap[:, :384]), in1=facv, op=ALU.mult),
                  reads=[ops, facb], writes=[self.ya])
        else:
            cx.op("dve", lambda e: e.tensor_tensor(out=self._v3(self.yt.ap[:]), in0=self._v3(ops.ap[:, :384]), in1=facv,
                                                   op=ALU.mult), reads=[ops, facb], writes=[self.yt])
            cx.op("pool", lambda e: e.tensor_tensor(out=self.ya.ap[:, 384 * g:384 * (g + 1)], in0=self.ya.ap[:, 384 * g:384 * (g + 1)],
                                                    in1=self.yt.ap[:], op=ALU.add), reads=[self.yt, self.ya], writes=[self.ya])

    def qblock(self, i, PT, ptbuf, KT, ktbuf, VA, vabuf, yT, ytbuf):
        cx, cm, T = self.cx, self.cm, self.T
        PS = cm.PS
        NBLK = self.NBLK
        I3 = 4 * i + 3
        cx.mute = False
        cs = slice(i * 128, (i + 1) * 128)
        ones = cm.ones

        def fmload(buf, dst, name, n=None):
            c0, k = FM[name]
            k = n or k
            cx.dma("sp", dst, r_(PT[c0:c0 + k, :, cs].rearrange("c p t -> p c t")), reads=[ptbuf], writes=[buf])

        fmload(self.qa, self.qa.ap[:].rearrange("p (c t) -> p c t", c=6), "qa")
        fmload(self.qb, self.qb.ap[:], "qb")
        fmload(self.qm, self.qm.ap[:], "qm")
        cx.op("dve", lambda e: e.tensor_copy(out=self.qbm.ap[0:64, :, 0, :], in_=self.qb.ap[0:64, :, :]),
              reads=[self.qb], writes=[self.qbm])
        cx.op("dve", lambda e: e.tensor_copy(out=self.qbm.ap[64:128, :, 1, :], in_=self.qb.ap[64:128, :, :]),
              reads=[self.qb], writes=[self.qbm])
        cx.dma("sp", self.gn.ap[:], r_(PT[FM["gn"][0], 0:32, cs]), reads=[ptbuf], writes=[self.gn])
        cx.mute = "gates" not in PARTS
        for k in range(18):
            ps = PS[k // 4]
            cx.op("pe", lambda e: e.matmul(ps.ap[:, (k % 4) * 128:(k % 4 + 1) * 128], T["esel"].ap[:, k * 128:(k + 1) * 128],
                                           self.gn.ap[:], start=True, stop=True), reads=[T["esel"], self.gn], writes=[ps],
                  signal=(k % 4 == 3 or k == 17))
        for b in range(5):
            n = 4 if b < 4 else 2
            cx.op("act", lambda e: e.activation(out=self.G.ap[:, 4 * b:4 * b + n, :].rearrange("p a q -> p (a q)"),
                                                in_=PS[b].ap[:, :n * 128], func=AF.Copy), reads=[PS[b]], writes=[self.G])
        cx.mute = "cmp" not in PARTS
        nch = min(self.NCC, (8 * I3 + 6) // 128 + 1)
        for g in range(2):
            qg = self.qa.ap[:, 384 * g:384 * (g + 1)]
            den, ops, scp = PS[4], PS[5], PS[6]
            cx.mute = "cmp" not in PARTS
            for ch in range(nch):
                sp_ = PS[ch % 2]
                cx.op("pe", lambda e: e.matmul(sp_.ap[:, :384], self.kcmpT.ap[:, g, ch * 128:(ch + 1) * 128], qg,
                                               start=True, stop=True), reads=[self.kcmpT, self.qa], writes=[sp_])
                u = I3 - 16 * ch
                P = self.Pc[ch]
                for r in range(3):
                    col = u * 6 + 3 * g + r
                    cx.op("act", lambda e: e.activation(out=P.ap[:, r * 128:(r + 1) * 128], in_=sp_.ap[:, r * 128:(r + 1) * 128],
                                                        func=AF.Exp, scale=SC_NSA, bias=T["cmpb"].ap[:, col:col + 1]),
                          reads=[sp_, T["cmpb"]], writes=[P])
                if u <= 19:
                    cx.op("dve", lambda e: e.tensor_tensor(out=self._v3(P.ap[:]), in0=self._v3(P.ap[:]),
                                                           in1=self._b3(T["cmpm"].ap[:, u * 128:(u + 1) * 128]), op=ALU.mult),
                          reads=[P, T["cmpm"]], writes=[P])
                cx.op("pe", lambda e: e.matmul(den.ap[:, :384], ones.ap[:], P.ap[:], start=(ch == 0), stop=(ch == nch - 1)),
                      reads=[ones, P], writes=[den])
            cx.op("dve", lambda e: e.tensor_scalar_max(out=self.rd.ap[:, :384], in0=den.ap[:, :384], scalar1=1e-36),
                  reads=[den], writes=[self.rd])
            cx.op("dve", lambda e: e.reciprocal(out=self.rd.ap[:, :384], in_=self.rd.ap[:, :384]),
                  reads=[self.rd], writes=[self.rd])
            for ch in range(nch):
                P = self.Pc[ch]
                cx.op("dve", lambda e: e.tensor_tensor(out=P.ap[:], in0=P.ap[:], in1=self.rd.ap[:, :384], op=ALU.mult),
                      reads=[P, self.rd], writes=[P])
                cx.op("pe", lambda e: e.matmul(ops.ap[:, :384], self.vcmp.ap[:, ch, g, :], P.ap[:], start=(ch == 0),
                                               stop=(ch == nch - 1)), reads=[self.vcmp, P], writes=[ops])
                for r in range(3):
                    cx.op("pe", lambda e: e.matmul(scp.ap[:, :NBLK], P.ap[:, r * 128:(r + 1) * 128],
                                                   T["wsel"].ap[:, ch * NBLK:(ch + 1) * NBLK],
                                                   start=(ch == 0 and r == 0), stop=(ch == nch - 1 and r == 2)),
                          reads=[T["wsel"], P], writes=[scp])
            cx.mute = "cmp" not in PARTS
            self._branch_out(g, 0, None, ops, True)
            cx.mute = "sel" not in PARTS
            x0 = NBLK - 8 * i
            fin, fin2, mx = self.fin, self.fin2, self.mx
            cx.op("dve", lambda e: e.tensor_tensor(out=fin.ap[:], in0=scp.ap[:, :NBLK], in1=T["m1"].ap[:, x0:x0 + NBLK], op=ALU.mult),
                  reads=[scp, T["m1"]], writes=[fin])
            cx.op("dve", lambda e: e.tensor_tensor(out=fin.ap[:], in0=fin.ap[:], in1=T["m2"].ap[:, x0:x0 + NBLK], op=ALU.add),
                  reads=[fin, T["m2"]], writes=[fin])
            cx.op("dve", lambda e: e.memset(fin.ap[:, 0:1], 1.0e9), reads=[fin], writes=[fin])
            cx.op("dve", lambda e: e.max(out=mx.ap[:, 0:8], in_=fin.ap[:]), reads=[fin], writes=[mx])
            cx.op("dve", lambda e: e.match_replace(out=fin2.ap[:], in_to_replace=mx.ap[:, 0:8], in_values=fin.ap[:],
                                                   imm_value=-3.0e38), reads=[fin, mx], writes=[fin2])
            cx.op("dve", lambda e: e.max(out=mx.ap[:, 8:16], in_=fin2.ap[:]), reads=[fin2, mx], writes=[mx])
            cx.op("dve", lambda e: e.tensor_scalar(out=self.sel[g].ap[:], in0=fin.ap[:], scalar1=mx.ap[:, 15:16], scalar2=None,
                                                   op0=ALU.is_ge), reads=[fin, mx], writes=[self.sel[g]])
            KB = self.KB
            for hf in range(max(1, NBLK // 128)):
                cx.op("pe", lambda e: e.matmul(PS[7].ap[:KB, :128], self.sel[g].ap[:, hf * KB:(hf + 1) * KB], T["ident"].ap[:],
                                               start=True, stop=True), reads=[self.sel[g], T["ident"]], writes=[PS[7]])
                cx.op("dve", lambda e: e.tensor_scalar(out=self.negs[g].ap[:KB, hf, :].rearrange("p (r q) -> p r q", r=3),
                                                       in0=PS[7].ap[:KB, :128].unsqueeze(1).to_broadcast([KB, 3, 128]),
                                                       scalar1=-1.0, scalar2=30000.0, op0=ALU.add, op1=ALU.mult),
                      reads=[PS[7]], writes=[self.negs[g]])
        cx.mute = False
        for br, (kname, vcol, k_lo) in enumerate((("ks", 0, 0), ("kw", 256, max(0, I3 - 7)))):
            dens = (PS[4], PS[5])
            opss = (PS[6], PS[7])
            cx.mute = ("slc", "win")[br] not in PARTS
            kcs = list(range(k_lo, I3 + 1))
            units = [(kc, g) for kc in kcs for g in range(2)]
            tiles = {}
            k0 = FM_K[kname]
            KB = self.KB

            def load(kc):
                rk, loc = kc % 4, kc // 4
                kt, vt = self.kd[self.ki % 6], self.vt[self.ki % 6]
                self.ki += 1
                cx.dma("sp", kt.ap[:], r_(KT[rk, k0:k0 + 2, :, loc * 128:(loc + 1) * 128].rearrange("g p t -> p g t")),
                       reads=[ktbuf], writes=[kt])
                cx.dma("sp", vt.ap[:, 0, :], r_(VA[rk, loc * 128:(loc + 1) * 128, vcol:vcol + 256]), reads=[vabuf], writes=[vt])
                tiles[kc] = (kt, vt)

            def front(u):
                kc, g = u
                kt = tiles[kc][0]
                qg = self.qa.ap[:, 384 * g:384 * (g + 1)]
                sp_ = PS[(2 * kc + g) % 4]
                cx.op("pe", lambda e: e.matmul(sp_.ap[:, :384], kt.ap[:, g, :], qg, start=True, stop=(br == 1)),
                      reads=[kt, self.qa], writes=[sp_], signal=(br == 1))
                if br == 0:
                    hf, kcl = (2 * kc) // KB, kc % (KB // 2)
                    cx.op("pe", lambda e: e.matmul(sp_.ap[:, :384], T["efull"].ap[:KB, kcl * 128:(kcl + 1) * 128],
                                                   self.negs[g].ap[:KB, hf, :], start=False, stop=True),
                          reads=[T["efull"], self.negs[g]], writes=[sp_])

            def back(u):
                kc, g = u
                m = I3 - kc
                vt = tiles[kc][1]
                sp_ = PS[(2 * kc + g) % 4]
                P = self.Ps[(2 * kc + g) % 6]
                for r in range(3):
                    col = m * 6 + 3 * g + r
                    cx.op("act", lambda e: e.activation(out=P.ap[:, r * 128:(r + 1) * 128], in_=sp_.ap[:, r * 128:(r + 1) * 128],
                                                        func=AF.Exp, scale=SC_NSA, bias=T["slcb"].ap[:, col:col + 1]),
                          reads=[sp_, T["slcb"]], writes=[P])
                Pv = self._v3(P.ap[:, :384])
                if br == 0 and m <= 3:
                    cx.op("dve", lambda e: e.tensor_tensor(out=Pv, in0=Pv, in1=self._b3(T["trim"].ap[:, m * 128:(m + 1) * 128]), op=ALU.mult),
                          reads=[P, T["trim"]], writes=[P])
                if br == 1:
                    cx.op("dve", lambda e: e.tensor_tensor(out=Pv, in0=Pv, in1=self._b3(T["winm"].ap[:, m * 128:(m + 1) * 128]), op=ALU.mult),
                          reads=[P, T["winm"]], writes=[P])
                cx.op("pe", lambda e: e.matmul(dens[g].ap[:, :384], ones.ap[:], P.ap[:, :384], start=(kc == k_lo), stop=(kc == I3)),
                      reads=[ones, P], writes=[dens[g]])
                cx.op("pe", lambda e: e.matmul(opss[g].ap[:, :384], vt.ap[:, 0, g * 128:(g + 1) * 128], P.ap[:, :384],
                                               start=(kc == k_lo), stop=(kc == I3)), reads=[vt, P], writes=[opss[g]])

            LK = 3
            for kc in kcs[:LK]:
                load(kc)
            for u in units[:2]:
                front(u)
            for ui, u in enumerate(units):
                if u[1] == 0:
                    pos = u[0] - k_lo
                    if pos + LK < len(kcs):
                        load(kcs[pos + LK])
                if ui + 2 < len(units):
                    front(units[ui + 2])
                back(u)
            for g in range(2):
                self._branch_out(g, 1 + br, dens[g], opss[g], False)
        cx.mute = False
        cx.dma("pool", yT[0:6, :, cs].rearrange("c p t -> p c t"), self.ya.ap[:].rearrange("p (c t) -> p c t", c=6),
               reads=[self.ya], writes=[ytbuf])
        cx.mute = "dil" not in PARTS
        obs, dbs = (PS[4], PS[5]), (PS[6], PS[7])
        items = [(g, m) for g in range(3) for m in range(DILN[g]) if I3 - m >= 0]
        dt_ = {}

        def dload(ii):
            g, m = items[ii]
            kc = I3 - m
            rk, loc = kc % 4, kc // 4
            kt, vt = self.kd[self.ki % 6], self.vd[self.di % 4]
            self.ki += 1
            self.di += 1
            k0 = FM_K["kb"] + 2 * g
            cx.dma("sp", kt.ap[:], r_(KT[rk, k0:k0 + 2, :, loc * 128:(loc + 1) * 128].rearrange("g p t -> p g t")),
                   reads=[ktbuf], writes=[kt])
            vsrc = VA[rk, loc * 128:(loc + 1) * 128, 512 + g * 256:512 + g * 256 + 256].rearrange("k (a b) -> k a b", b=128)
            for par in range(2):
                cx.dma("sp", vt.ap[:, par::2, par * 64:par * 64 + 64], r_(vsrc[:, :, par * 64:par * 64 + 64]),
                       reads=[vabuf], writes=[vt])
            dt_[ii] = (kt, vt)

        def dfront(ii):
            g, m = items[ii]
            kt = dt_[ii][0]
            sp_ = PS[ii % 4]
            for h in range(4):
                cx.op("pe", lambda e: e.matmul(sp_.ap[:, h * 128:(h + 1) * 128], kt.ap[:, h // 2, :],
                                               self.qbm.ap[:, 2 * g + h // 2, h % 2, :], start=True, stop=True),
                      reads=[kt, self.qbm], writes=[sp_], signal=(h == 3))

        def dback(ii):
            g, m = items[ii]
            vt = dt_[ii][1]
            sp_ = PS[ii % 4]
            P = self.Ps[ii % 6]
            for h in range(4):
                col = (DIL_OFF2[g] + m) * 4 + h
                cx.op("act", lambda e: e.activation(out=P.ap[:, h * 128:(h + 1) * 128], in_=sp_.ap[:, h * 128:(h + 1) * 128],
                                                    func=AF.Exp, scale=SC_DIL, bias=T["dilb"].ap[:, col:col + 1]),
                      reads=[sp_, T["dilb"]], writes=[P])
            kd_ = DIL_OFF2[g] + m
            P4 = P.ap[:].rearrange("p (h q) -> p h q", h=4)
            cx.op("dve", lambda e: e.tensor_tensor(out=P4, in0=P4, in1=T["dilk"].ap[:, kd_ * 128:(kd_ + 1) * 128].unsqueeze(1).to_broadcast([128, 4, 128]),
                                                   op=ALU.mult), reads=[P, T["dilk"]], writes=[P])
            cx.op("dve", lambda e: e.tensor_tensor(out=P.ap[:], in0=P.ap[:], in1=T["dilr"].ap[:, g * 512:(g + 1) * 512], op=ALU.mult),
                  reads=[P, T["dilr"]], writes=[P])
            first, last = (ii == 0), (ii == len(items) - 1)
            for h in range(4):
                pr = h // 2
                cx.op("pe", lambda e: e.matmul(obs[pr].ap[:, 0:128], vt.ap[:, h, :], P.ap[:, h * 128:(h + 1) * 128],
                                               start=(first and h % 2 == 0), stop=(last and h % 2 == 1)),
                      reads=[vt, P], writes=[obs[pr]], signal=False)
                cx.op("pe", lambda e: e.matmul(dbs[pr].ap[:, 0:128], T["hones"].ap[:, (h % 2) * 128:(h % 2 + 1) * 128],
                                               P.ap[:, h * 128:(h + 1) * 128], start=(first and h % 2 == 0), stop=(last and h % 2 == 1)),
                      reads=[T["hones"], P], writes=[dbs[pr]], signal=(h == 3))

        n_it = len(items)
        for ii in range(min(3, n_it)):
            dload(ii)
        for ii in range(min(2, n_it)):
            dfront(ii)
        for ii in range(n_it):
            if ii + 3 < n_it:
                dload(ii + 3)
            if ii + 2 < n_it:
                dfront(ii + 2)
            dback(ii)
        for pr in range(2):
            cx.op("dve", lambda e: e.reciprocal(out=self.rd.ap[:, pr * 128:(pr + 1) * 128], in_=dbs[pr].ap[:, :128]),
                  reads=[dbs[pr]], writes=[self.rd])
            cx.op("dve", lambda e: e.tensor_tensor(out=self.yb.ap[:, pr * 128:(pr + 1) * 128], in0=obs[pr].ap[:, :128],
                                                   in1=self.rd.ap[:, pr * 128:(pr + 1) * 128], op=ALU.mult),
                  reads=[obs[pr], self.rd], writes=[self.yb])
        cx.dma("pool", yT[6:8, :, cs].rearrange("c p t -> p c t"), self.yb.ap[:].rearrange("p (c t) -> p c t", c=2),
               reads=[self.yb], writes=[ytbuf])
        cx.mute = "mem" not in PARTS
        om, dm = PS[6], PS[7]
        for mc in range(2):
            sp_ = PS[mc]
            for h in range(4):
                cx.op("pe", lambda e: e.matmul(sp_.ap[:, h * 128:(h + 1) * 128], self.mkT.ap[:, h, mc * 128:(mc + 1) * 128],
                                               self.qm.ap[:, h, :], start=True, stop=True),
                      reads=[self.mkT, self.qm], writes=[sp_], signal=(h == 3))
            P = self.Ps[mc]
            cx.op("act", lambda e: e.activation(out=P.ap[:], in_=sp_.ap[:], func=AF.Exp, scale=SC_NSA), reads=[sp_], writes=[P])
        for h in range(4):
            for mc in range(2):
                P = self.Ps[mc]
                cx.op("pe", lambda e: e.matmul(om.ap[:, h * 128:(h + 1) * 128], self.mv.ap[:, mc, h * 128:(h + 1) * 128],
                                               P.ap[:, h * 128:(h + 1) * 128], start=(mc == 0), stop=(mc == 1)),
                      reads=[self.mv, P], writes=[om], signal=(mc == 1))
        for h in range(4):
            for mc in range(2):
                P = self.Ps[mc]
                cx.op("pe", lambda e: e.matmul(dm.ap[:, h * 128:(h + 1) * 128], ones.ap[:], P.ap[:, h * 128:(h + 1) * 128],
                                               start=(mc == 0), stop=(mc == 1)), reads=[ones, P], writes=[dm], signal=(mc == 1))
        cx.op("dve", lambda e: e.reciprocal(out=self.rd.ap[:], in_=dm.ap[:]), reads=[dm], writes=[self.rd])
        cx.op("dve", lambda e: e.tensor_tensor(out=self.ym.ap[:], in0=om.ap[:], in1=self.rd.ap[:], op=ALU.mult),
              reads=[om, self.rd], writes=[self.ym])
        cx.dma("pool", yT[8:12, :, cs].rearrange("c p t -> p c t"), self.ym.ap[:].rearrange("p (c t) -> p c t", c=4),
               reads=[self.ym], writes=[ytbuf])
        cx.mute = False


def _dt(nc, name, shape, kind):
    return nc.dram_tensor(name, list(shape), F32, kind=kind).ap()


def build_AC(S, do_C, do_A, last):
    NT = S // 4
    nc = bass.Bass("TRN2", target_bir_lowering=False)
    nc.dge_precook = False
    I_, O_ = "ExternalInput", "ExternalOutput"
    h_in = _dt(nc, "h_in", (NCH, 128, NT), I_)
    gv = _dt(nc, "gv", (128, 6 * NCH), I_)
    if do_C:
        yT = _dt(nc, "yT", (12, 128, NT), I_)
        gates = _dt(nc, "gates", (48, 128, NT), I_)
        wup = _dt(nc, "wup", (1536, D), I_)
        wout = _dt(nc, "wout", (D, D), I_)
        f2g, f2u, f2d = _dt(nc, "f2g", (D, DFF), I_), _dt(nc, "f2u", (D, DFF), I_), _dt(nc, "f2d", (DFF, D), I_)
        h2 = _dt(nc, "h2s", (NCH, 128, NT), "Internal")
    if do_A:
        f1g, f1u, f1d = _dt(nc, "f1g", (D, DFF), I_), _dt(nc, "f1u", (D, DFF), I_), _dt(nc, "f1d", (DFF, D), I_)
        winf = _dt(nc, "winf", (D, NFM * 128), I_)
        winv = _dt(nc, "winv", (D, NVM), I_)
        h1o = _dt(nc, "h1o", (NCH, 128, NT), O_)
        pto = _dt(nc, "pto", (NFM, 128, NT), O_)
        pvo = _dt(nc, "pvo", (NT, NVM), O_)
    if do_C:
        h3 = _dt(nc, "h3o", (NCH, 128, NT), O_ if not do_A else "Internal")
    with ExitStack() as es:
        cx = Ctx(nc, es)
        cm = Common(cx, None)
        g = Buf(cx.sb("g", (128, 6 * NCH), F32))
        cx.dma("sp", g.ap[:], gv, writes=[g])
        for k in (2, 4):
            cx.op("dve", lambda e: e.tensor_single_scalar(out=g.ap[:, k * NCH:(k + 1) * NCH], in_=g.ap[:, k * NCH:(k + 1) * NCH],
                                                          scalar=0.5, op=ALU.mult), reads=[g], writes=[g])
        G = lambda k: g.ap[:, k * NCH:(k + 1) * NCH]
        ffn = FFN(cx, cm)
        bin_, b2, b3, b1, bpt, bpv, by, bg = (Buf() for _ in range(8))
        for t in range(NT // TT):
            ts = slice(t * TT, (t + 1) * TT)
            src, srcb = h_in[:, :, ts], bin_
            insb = False
            if do_C:
                merge_out(ffn, yT[:, :, ts], by, gates[:, :, ts], bg, wup, wout)
                ffn.post(src, srcb, h2[:, :, ts], b2, G(0), g)
                ffn.run(h2[:, :, ts], b2, h3[:, :, ts], b3, f2g, f2u, f2d, G(1), G(2), g, final=not do_A, h_in_sbuf=True)
                src, srcb, insb = h3[:, :, ts], b3, True
            if do_A:
                ffn.run(src, srcb, h1o[:, :, ts], b1, f1g, f1u, f1d, G(3), G(4), g, final=True, h_in_sbuf=insb)
                prenorm(ffn, G(5), g)
                proj(ffn, winf, winv, pto[:, :, ts], pvo[ts, :], bpt, bpv)
        cx.finish()
    return nc


TAB_SHAPES = None


def build_B(S):
    NT = S // 4
    nc = bass.Bass("TRN2", target_bir_lowering=False)
    nc.dge_precook = False
    I_, O_ = "ExternalInput", "ExternalOutput"
    PT = _dt(nc, "PT", (NFM, 128, NT), I_)
    KT = _dt(nc, "KT", (4, 14, 128, NT), I_)
    VA = _dt(nc, "VA", (4, NT, NVM), I_)
    memT = _dt(nc, "memT", (NCH, 128, 256), I_)
    wkv = _dt(nc, "wkv", (D, 1024), I_)
    gm = _dt(nc, "gmem", (128, NCH), I_)
    pek, pev = _dt(nc, "pek", (128, 64), I_), _dt(nc, "pev", (128, 64), I_)
    kw1, kw2 = _dt(nc, "kw1", (4096, 256), I_), _dt(nc, "kw2", (256, 128), I_)
    vw1, vw2 = _dt(nc, "vw1", (4096, 256), I_), _dt(nc, "vw2", (256, 128), I_)
    tabs_np = attn_tables(S, 0)
    tabs = {k: (nc.dram_tensor("t_" + k, list(v.shape), BF16, kind=I_).ap() if k == "efull" else _dt(nc, "t_" + k, v.shape, I_))
            for k, v in tabs_np.items()}
    yT = _dt(nc, "yT", (12, 128, NT), O_)
    with ExitStack() as es:
        cx = Ctx(nc, es)
        cm = Common(cx, None)
        at = Attn(cx, cm, S, tabs)
        g = Buf(cx.sb("gmem_sb", (128, NCH), F32))
        cx.dma("sp", g.ap[:], gm, writes=[g])
        bpt, bkt, bva, byt = (Buf() for _ in range(4))
        with ExitStack() as es2:
            nscr = max(NCH * 256 + NCH * 512, S + 16 + 32 * 256 + 256 + 64 + 2 * at.NCC * 128)
            scr = Buf(es2.enter_context(nc.sbuf_tensor("scr", [128, nscr], F32R)))
            at.scf = Buf(es2.enter_context(nc.sbuf_tensor("scf", [128, NCH * 256 + 256], F32)))
            if "mem0" in PARTS:
                at.prep_mem(memT, wkv, g.ap, g, scr)
            cx.barrier()
            if "cmp0" in PARTS:
                at.prep_cmp(KT, bkt, 0, pek, kw1, kw2, scr)
            cx.barrier()
            if "cmp0" in PARTS:
                at.prep_cmp(KT, bkt, 1, pev, vw1, vw2, scr)
            cx.barrier()
        at.alloc_q()
        for i in range(at.NQ):
            at.qblock(i, PT, bpt, KT, bkt, VA, bva, yT, byt)
        cx.finish()
    return nc


IN_NAMES = ("q_a", "kc", "vc", "ks", "vs", "kw", "vw", "g_nsa", "q_b", "k_b", "v_b", "q_m", "g_a", "g_b", "g_m")
IN_SIZES = (768, 256, 256, 256, 256, 256, 256, 18, 768, 768, 768, 512, 2048, 2048, 2048)
_PROGS = {}
_DBG = {}
PARTS = {"mem0", "cmp0", "gates", "cmp", "sel", "slc", "win", "dil", "mem"}


def _prog(key, fn):
    if key not in _PROGS:
        _PROGS[key] = fn()
    return _PROGS[key]


def _fm(a2d):
    return np.ascontiguousarray(a2d.T).reshape(NCH, 128, -1)


def _gt(vec):
    return vec.reshape(NCH, 128).T


def _split_win(w):
    offs = np.cumsum((0,) + IN_SIZES)
    col = {n: w[:, offs[i]:offs[i + 1]] for i, n in enumerate(IN_NAMES)}
    gn = np.zeros((D, 128), np.float32)
    gn[:, :18] = col["g_nsa"]
    winf = np.concatenate([col["q_a"], col["kc"], col["vc"], col["ks"], col["kw"], col["q_b"], col["k_b"], col["q_m"],
                           gn, col["g_a"], col["g_b"], col["g_m"]], axis=1)
    winv = np.concatenate([col["vs"], col["vw"], col["v_b"]], axis=1)
    return np.ascontiguousarray(winf), np.ascontiguousarray(winv)


def _run(nc, maps):
    res = run_bass_kernel_spmd(nc, maps, core_ids=list(range(8)))
    return res.results


def kernel(**inp):
    x = np.asarray(inp["x"], np.float32)
    B, S, _ = x.shape
    NT, NI = S // 4, S // 128
    L = inp["w_in"].shape[0]
    P = {k: np.asarray(v, np.float32) for k, v in inp.items()}

    def shard(xb, j):
        return xb.reshape(NI // 4, 4, 128, D)[:, j].reshape(NT, D)

    h = [_fm(shard(x[c // 4], c % 4)) for c in range(8)]
    memT = [np.ascontiguousarray(P["mem"][b].T).reshape(NCH, 128, 256) for b in range(B)]
    tabs = [attn_tables(S, j) for j in range(4)]
    wins = [_split_win(P["w_in"][l]) for l in range(L)]

    def gv(lc, la):
        cols = []
        for nm, l in (("mix_post_g", lc), ("ffn2_pre_g", lc), ("ffn2_post_g", lc), ("ffn1_pre_g", la), ("ffn1_post_g", la),
                      ("mix_pre_g", la)):
            cols.append(_gt(P[nm][min(max(l, 0), L - 1)]))
        return np.ascontiguousarray(np.concatenate(cols, axis=1))

    def a_inputs(l):
        return {"f1g": P["ffn1_w_gate"][l], "f1u": P["ffn1_w_up"][l], "f1d": P["ffn1_w_down"][l],
                "winf": wins[l][0], "winv": wins[l][1]}

    def c_inputs(l, yT, pto):
        g0 = FM["ga"][0]
        return [{"yT": yT[c], "gates": np.ascontiguousarray(pto[c][g0:g0 + 48]),
                 "wup": np.ascontiguousarray(np.concatenate([P["w_up_nsa"][l], P["w_up_dil"][l], P["w_up_mem"][l]], axis=0)),
                 "wout": P["w_out"][l], "f2g": P["ffn2_w_gate"][l], "f2u": P["ffn2_w_up"][l], "f2d": P["ffn2_w_down"][l]}
                for c in range(8)]

    ncA = _prog(("AC", S, False, True), lambda: build_AC(S, False, True, False))
    base = a_inputs(0)
    g_ = gv(0, 0)
    res = _run(ncA, [dict(base, h_in=h[c], gv=g_) for c in range(8)])
    for l in range(L):
        h1 = [r["h1o"] for r in res]
        _DBG["h1_%d" % l] = h1
        pto = [r["pto"] for r in res]
        pvo = [r["pvo"] for r in res]
        ksel = np.concatenate([np.arange(FM[n][0], FM[n][0] + FM[n][1]) for n in KSIDE])
        KT = [np.ascontiguousarray(np.stack([pto[4 * b + j][ksel] for j in range(4)])) for b in range(B)]
        VA = [np.ascontiguousarray(np.stack([pvo[4 * b + j] for j in range(4)])) for b in range(B)]
        ncB = _prog(("B", S), lambda: build_B(S))
        pe2 = lambda a: np.ascontiguousarray(np.repeat(a.T, 2, axis=1))
        maps = []
        for c in range(8):
            b, j = c // 4, c % 4
            m = {"PT": pto[c], "KT": KT[b], "VA": VA[b], "memT": memT[b], "wkv": P["w_mem_kv"][l],
                 "gmem": np.ascontiguousarray(_gt(P["mem_norm_g"][l])), "pek": pe2(P["cmp_pe_k"][l]), "pev": pe2(P["cmp_pe_v"][l]),
                 "kw1": P["cmp_k_w1"][l], "kw2": P["cmp_k_w2"][l], "vw1": P["cmp_v_w1"][l], "vw2": P["cmp_v_w2"][l]}
            for k, v in tabs[j].items():
                m["t_" + k] = v
            maps.append(m)
        resB = _run(ncB, maps)
        yT = [r["yT"] for r in resB]
        _DBG["yT_%d" % l] = yT
        _DBG["pto_%d" % l] = pto
        last = (l == L - 1)
        ncC = _prog(("AC", S, True, not last), lambda: build_AC(S, True, not last, last))
        cin = c_inputs(l, yT, pto)
        g_ = gv(l, l + 1)
        extra = {} if last else a_inputs(l + 1)
        res = _run(ncC, [dict(cin[c], h_in=h1[c], gv=g_, **extra) for c in range(8)])
    out = np.zeros((B, S, D), np.float32)
    for c in range(8):
        o = res[c]["h3o"].reshape(D, NT).T.reshape(NI // 4, 128, D)
        out[c // 4].reshape(NI // 4, 4, 128, D)[:, c % 4] = o
    return out
```
